# Optimizing a Trainium2 kernel written in Bass

```python
import math
import jax, jax.numpy as jnp
from jax import lax
import numpy as np

D_MODEL = 1024
BATCH = 2
SEQ = 8192
DEPTH = 2
DEC_BATCH = 4
DEC_SEQ = 4096
PAST_LEN = 128

N_META = 16
GRID_W = 64
MIX_WIDTH = D_MODEL
POOL_WIDTH = MIX_WIDTH // 2
POOL_WINDOWS = (2, 4, 8, 16)
N_POOL_GROUPS = len(POOL_WINDOWS)
POOL_GROUP = POOL_WIDTH // N_POOL_GROUPS
HEAD_DIM = 64
N_HEADS = (MIX_WIDTH - POOL_WIDTH) // HEAD_DIM
N_KV_HEADS = 2
Q_PER_KV = N_HEADS // N_KV_HEADS
ATTN_WIDTH = N_HEADS * HEAD_DIM
KV_WIDTH = N_KV_HEADS * HEAD_DIM
IN_WIDTH = POOL_WIDTH + ATTN_WIDTH + 2 * KV_WIDTH
ROT_PAIRS = HEAD_DIM // 4
ROPE_THETA = 10000.0
Q_BLOCK = 128
D_FF = 4 * D_MODEL
EPS = 1e-6

kernel_name = "hybrid_pool_gqa_encoder"


def rmsnorm(x, g):
    xf = x.astype(jnp.float32)
    y = xf * lax.rsqrt(jnp.mean(xf * xf, axis=-1, keepdims=True) + EPS)
    return (y * g.astype(jnp.float32)).astype(x.dtype)


def rope_tables(n_tok):
    rows = n_tok // GRID_W
    r = jnp.concatenate([jnp.zeros((N_META,), jnp.int32),
                         jnp.repeat(jnp.arange(rows, dtype=jnp.int32), GRID_W)]).astype(jnp.float32)
    c = jnp.concatenate([jnp.zeros((N_META,), jnp.int32),
                         jnp.tile(jnp.arange(GRID_W, dtype=jnp.int32), rows)]).astype(jnp.float32)
    inv = ROPE_THETA ** (-jnp.arange(ROT_PAIRS, dtype=jnp.float32) / ROT_PAIRS)
    ang = jnp.stack([r[:, None] * inv, c[:, None] * inv], axis=1)
    return jnp.cos(ang), jnp.sin(ang)


def apply_rope(x, cos, sin):
    B, L, H, _ = x.shape
    xf = x.astype(jnp.float32).reshape(B, L, H, 2, 2, ROT_PAIRS)
    x1, x2 = xf[..., 0, :], xf[..., 1, :]
    c = cos[None, :, None]
    s = sin[None, :, None]
    out = jnp.stack([x1 * c - x2 * s, x2 * c + x1 * s], axis=-2)
    return out.reshape(B, L, H, HEAD_DIM).astype(x.dtype)


def pool_mixer(u, w_pool, pool_scale):
    B, L, _ = u.shape
    uf = u.astype(jnp.float32)
    cs = jnp.concatenate([jnp.zeros((B, 1, POOL_WIDTH), jnp.float32), jnp.cumsum(uf, axis=1)], axis=1)
    csg = cs.reshape(B, L + 1, N_POOL_GROUPS, POOL_GROUP)
    ug = uf.reshape(B, L, N_POOL_GROUPS, POOL_GROUP)
    t = jnp.arange(L, dtype=jnp.int32)
    outs = []
    for gi, w in enumerate(POOL_WINDOWS):
        lo = jnp.clip(t - w // 2, 0, L)
        hi = jnp.clip(t + (w - w // 2), 0, L)
        cnt = (hi - lo).astype(jnp.float32)[None, :, None]
        mean = (csg[:, hi, gi] - csg[:, lo, gi]) / cnt
        outs.append(mean - ug[:, :, gi])
    d = jnp.stack(outs, axis=2).astype(u.dtype)
    y = jnp.einsum('blgc,gce->blge', d, w_pool).reshape(B, L, POOL_WIDTH)
    return y * pool_scale


def block_attention(q, k, v):
    B, L = q.shape[0], q.shape[1]
    scale = 1.0 / math.sqrt(HEAD_DIM)

    def attend(qb):
        s = jnp.einsum('bqkgd,bskd->bkgqs', qb, k, preferred_element_type=jnp.float32) * scale
        p = jax.nn.softmax(s, axis=-1).astype(v.dtype)
        return jnp.einsum('bkgqs,bskd->bqkgd', p, v)

    o_meta = attend(q[:, :N_META])
    n_tok = L - N_META
    n_blk = n_tok // Q_BLOCK
    qr = q[:, N_META:].reshape(B, n_blk, Q_BLOCK, N_KV_HEADS, Q_PER_KV, HEAD_DIM)
    o_real = lax.map(attend, jnp.moveaxis(qr, 1, 0))
    o_real = jnp.moveaxis(o_real, 0, 1).reshape(B, n_tok, N_KV_HEADS, Q_PER_KV, HEAD_DIM)
    return jnp.concatenate([o_meta, o_real], axis=1)


def encoder_layer(h, cos, sin, norm1_g, w_in, q_norm_g, k_norm_g, w_pool, pool_scale,
                  w_out, norm2_g, w_mlp_in, w_mlp_out):
    B, L, _ = h.shape
    n = rmsnorm(h, norm1_g)
    proj = n @ w_in
    u_pool = proj[..., :POOL_WIDTH]
    q = proj[..., POOL_WIDTH:POOL_WIDTH + ATTN_WIDTH].reshape(B, L, N_HEADS, HEAD_DIM)
    k = proj[..., POOL_WIDTH + ATTN_WIDTH:POOL_WIDTH + ATTN_WIDTH + KV_WIDTH].reshape(B, L, N_KV_HEADS, HEAD_DIM)
    v = proj[..., POOL_WIDTH + ATTN_WIDTH + KV_WIDTH:].reshape(B, L, N_KV_HEADS, HEAD_DIM)
    pool_out = pool_mixer(u_pool, w_pool, pool_scale)
    q = apply_rope(rmsnorm(q, q_norm_g), cos, sin).reshape(B, L, N_KV_HEADS, Q_PER_KV, HEAD_DIM)
    k = apply_rope(rmsnorm(k, k_norm_g), cos, sin)
    attn_out = block_attention(q, k, v).reshape(B, L, ATTN_WIDTH)
    h = h + jnp.concatenate([pool_out, attn_out], axis=-1) @ w_out
    m = rmsnorm(h, norm2_g) @ w_mlp_in
    h = h + jnp.square(jax.nn.relu(m)) @ w_mlp_out
    return h


def run_trunk(x, meta_tokens, norm1_g, w_in, q_norm_g, k_norm_g, w_pool, pool_scale,
              w_out, norm2_g, w_mlp_in, w_mlp_out):
    B, n_tok, _ = x.shape
    cos, sin = rope_tables(n_tok)
    meta = jnp.broadcast_to(meta_tokens[None].astype(x.dtype), (B, N_META, D_MODEL))
    h = jnp.concatenate([meta, x], axis=1)
    for l in range(DEPTH):
        h = encoder_layer(h, cos, sin, norm1_g[l], w_in[l], q_norm_g[l], k_norm_g[l], w_pool[l],
                          pool_scale[l], w_out[l], norm2_g[l], w_mlp_in[l], w_mlp_out[l])
    return h[:, N_META:]


def setup_inputs(seed: int = 0) -> dict:
    key = jax.random.key(seed)
    ks = jax.random.split(key, 14)
    f32 = jnp.float32
    nrm = lambda k, shape, s: jax.random.normal(k, shape, f32) * s
    return {
        "x_prompt": nrm(ks[0], (BATCH, SEQ, D_MODEL), 1.0),
        "x_sample": nrm(ks[1], (DEC_BATCH, DEC_SEQ, D_MODEL), 1.0),
        "meta_tokens": nrm(ks[2], (N_META, D_MODEL), 1.0),
        "norm1_g": 1.0 + nrm(ks[3], (DEPTH, D_MODEL), 0.02),
        "w_in": nrm(ks[4], (DEPTH, D_MODEL, IN_WIDTH), D_MODEL ** -0.5),
        "q_norm_g": 1.0 + nrm(ks[5], (DEPTH, HEAD_DIM), 0.02),
        "k_norm_g": 1.0 + nrm(ks[6], (DEPTH, HEAD_DIM), 0.02),
        "w_pool": nrm(ks[7], (DEPTH, N_POOL_GROUPS, POOL_GROUP, POOL_GROUP), POOL_GROUP ** -0.5),
        "pool_scale": 1.0 + nrm(ks[8], (DEPTH, POOL_WIDTH), 0.1),
        "w_out": nrm(ks[9], (DEPTH, MIX_WIDTH, D_MODEL), MIX_WIDTH ** -0.5),
        "norm2_g": 1.0 + nrm(ks[10], (DEPTH, D_MODEL), 0.02),
        "w_mlp_in": nrm(ks[11], (DEPTH, D_MODEL, D_FF), D_MODEL ** -0.5),
        "w_mlp_out": nrm(ks[12], (DEPTH, D_FF, D_MODEL), 0.5 * D_FF ** -0.5),
    }


def reference(x_prompt, x_sample, meta_tokens, norm1_g, w_in, q_norm_g, k_norm_g, w_pool,
              pool_scale, w_out, norm2_g, w_mlp_in, w_mlp_out):
    y_prompt = run_trunk(x_prompt, meta_tokens, norm1_g, w_in, q_norm_g, k_norm_g, w_pool,
                         pool_scale, w_out, norm2_g, w_mlp_in, w_mlp_out)
    y_sample = run_trunk(x_sample, meta_tokens, norm1_g, w_in, q_norm_g, k_norm_g, w_pool,
                         pool_scale, w_out, norm2_g, w_mlp_in, w_mlp_out)
    return (y_prompt, y_sample)
```

```python
import numpy as np
import ml_dtypes
from contextlib import ExitStack
import concourse.bass as bass
import concourse.mybir as mybir
from concourse.bass_utils import run_bass_kernel_spmd

F32 = mybir.dt.float32
BF16 = mybir.dt.bfloat16
AF = mybir.ActivationFunctionType
ALU = mybir.AluOpType

D = 1024
DFF = 4096
NMETA = 16
GRID_W = 64
EPS = 1e-6
INW = 1280
NCORES = 8
ENGS = ("pe", "act", "dve", "pool", "sp")


class Op:
    __slots__ = ("eng", "fn", "deps", "is_dma", "key", "needed", "tok", "inc", "pos")


class Sched:
    def __init__(self):
        self.ops = {e: [] for e in ENGS}
        self.reg = {}
        self.dma_cnt = {}
        self.final = []

    def add(self, eng, fn, reads=(), writes=(), dma_key=None, pwrites=(), inc=16):
        op = Op()
        op.eng = eng
        op.fn = fn
        op.is_dma = dma_key is not None
        op.key = dma_key
        op.needed = False
        op.tok = None
        op.inc = inc
        psr = tuple(r for r in reads if r.startswith("ps") and r not in writes)
        reads = tuple(r for r in reads if not r.startswith("ps"))
        writes = tuple(writes) + psr
        deps = []
        for r in reads:
            st = self.reg.setdefault(r, [[], [], []])
            deps += st[0]
        for wname in writes:
            st = self.reg.setdefault(wname, [[], [], []])
            deps += st[0] + st[1] + st[2]
        for wname in pwrites:
            st = self.reg.setdefault(wname, [[], [], []])
            if st[1]:
                st[2] = st[2] + st[0] + st[1]
                st[0] = []
                st[1] = []
            deps += st[2]
        seen = set()
        op.deps = []
        for d in deps:
            if id(d) not in seen and d is not op:
                seen.add(id(d))
                op.deps.append(d)
        for r in reads:
            if r not in writes:
                self.reg[r][1].append(op)
        for wname in writes:
            self.reg[wname] = [[op], [], []]
        for wname in pwrites:
            self.reg[wname][0].append(op)
        if op.is_dma:
            c = self.dma_cnt.get(dma_key, 0) + 1
            self.dma_cnt[dma_key] = c
            op.tok = ("dma:" + dma_key, inc * c)
        self.ops[eng].append(op)
        return op

    def finalize(self):
        for e in ENGS:
            pos = {id(op): i for i, op in enumerate(self.ops[e])}
            for op in self.ops[e]:
                op.pos = pos[id(op)]
        for e in ENGS:
            for op in self.ops[e]:
                last = {}
                for d in op.deps:
                    if d.is_dma:
                        continue
                    if e == "pe" and d.eng == "pe":
                        continue
                    if d.eng not in last or d.pos > last[d.eng].pos:
                        last[d.eng] = d
                for d in last.values():
                    d.needed = True
        for e in ENGS:
            c = 0
            for op in self.ops[e]:
                if not op.is_dma and op.needed:
                    c += 1
                    op.tok = ("eng:" + e, c)
            nxt = None
            for op in reversed(self.ops[e]):
                if op.is_dma:
                    continue
                if op.needed:
                    nxt = op.tok
                elif op.tok is None:
                    op.tok = nxt
        names = ["eng:" + e for e in ENGS] + ["dma:" + k for k in self.dma_cnt]
        return names

    def emit(self, e, handle, sems):
        waited = {}
        for op in self.ops[e]:
            need = {}
            for d in op.deps:
                if e == "pe" and d.eng == "pe" and not d.is_dma:
                    continue
                s, v = d.tok
                if v > need.get(s, 0):
                    need[s] = v
            for s, v in need.items():
                if waited.get(s, 0) >= v:
                    continue
                handle.wait_ge(sems[s], v)
                waited[s] = v
            inst = op.fn(handle)
            if op.is_dma:
                inst.then_inc(sems[op.tok[0]], op.inc)
            elif op.needed:
                inst.then_inc(sems[op.tok[0]], 1)
        if e == "sp":
            for s, v in self.final:
                if waited.get(s, 0) < v:
                    handle.wait_ge(sems[s], v)
                    waited[s] = v


def build(TS):
    NB = TS // 512
    NT = NMETA + TS
    SL = [dict(n="p", R=4), dict(n="s", R=2)]
    nc = bass.Bass("TRN2", target_bir_lowering=False)
    S = Sched()

    def din(name, shape, dt=F32):
        return nc.dram_tensor(name, list(shape), dt, kind="ExternalInput").ap()

    def dout(name, shape, dt=F32):
        return nc.dram_tensor(name, list(shape), dt, kind="ExternalOutput").ap()

    def dint(name, shape, dt):
        return nc.dram_tensor(name, list(shape), dt)

    x_in = {"p": din("xp", [TS, D]), "s": din("xs", [TS, D])}
    meta_in = din("metaT", [D, NMETA])
    y_out = {"p": dout("yp", [TS, D]), "s": dout("ys", [TS, D])}
    w_in_f = din("w_in", [2, D, INW])
    w_out_f = din("w_out", [2, D, D])
    w1_f = din("w1", [2, D, DFF])
    w2_f = din("w2", [2, DFF, D])
    wp_f = din("w_pool", [2, 4, 128, 128])
    NPAR = 64
    par_in = din("params", [128, NPAR])
    rope_in = din("rope", [2, 2, 128, NT])
    icnt_in = din("icnt", [2, 128, 4, 32])
    ident_in = din("ident", [128, 128])
    rt_in = din("rt", [128, 128])
    cb_in = din("cbf", [128, 384], BF16)

    w_in_b = dint("w_in_b", [2, 10, 128, 8, 128], BF16).ap()
    w_in_bc = dint("w_in_bc", [2, D, INW], BF16).ap()
    w_out_b = dint("w_out_b", [2, D, D], BF16).ap()
    w1_b = dint("w1_b", [2, D, DFF], BF16).ap()
    w2_b = dint("w2_b", [2, DFF, D], BF16).ap()
    wp_b = dint("wp_b", [2, 4, 128, 128], BF16).ap()
    hT_d, qT_d, uT_d, umT_d = {}, {}, {}, {}
    kT_send, v_send, kT_gath, v_gath, kT_meta, v_meta, e_send, e_gath = {}, {}, {}, {}, {}, {}, {}, {}
    for l in range(2):
        for sl in SL:
            n, R = sl["n"], sl["R"]
            k = (l, n)
            hT_d[k] = dint(f"hT{l}{n}", [8, 128, NT], F32).ap()
            qT_d[k] = dint(f"qT{l}{n}", [4, 128, NT], BF16).ap()
            uT_d[k] = dint(f"uT{l}{n}", [4, 128, TS], F32).ap()
            umT_d[k] = dint(f"umT{l}{n}", [4, 128, 16], F32).ap()
            kT_send[k] = dint(f"kTs{l}{n}", [128, TS], BF16)
            v_send[k] = dint(f"vs{l}{n}", [TS, 130], BF16)
            kT_gath[k] = dint(f"kTg{l}{n}", [R * 128, TS], BF16)
            v_gath[k] = dint(f"vg{l}{n}", [R * TS, 130], BF16)
            kT_meta[k] = dint(f"kTm{l}{n}", [128, 16], BF16).ap()
            v_meta[k] = dint(f"vm{l}{n}", [16, 130], BF16).ap()
            e_send[k] = dint(f"es{l}{n}", [512, 16], F32)
            e_gath[k] = dint(f"eg{l}{n}", [R * 512, 16], F32)

    rec_d = dint("rec_d", [2, 512], F32).ap()
    es = ExitStack()
    sbtot = [0]

    def sb(name, shape, dt):
        n_ = 1
        for d_ in shape[1:]:
            n_ *= d_
        sbtot[0] += n_ * (2 if dt == BF16 else 4)
        return es.enter_context(nc.sbuf_tensor("s_" + name, list(shape), dt))

    ident = sb("ident", [128, 128], F32)
    rt_sb = sb("rt_sb", [128, 128], F32)
    cbf = sb("cbf", [128, 384], BF16)
    par = sb("par", [128, NPAR], F32)
    icnt = sb("icnt", [128, 2, 4, 32], F32)
    onesD = cbf[:, 0:128]
    blk64 = cbf[:, 128:256]
    ones_f = sb("ones_f", [128, 64], F32)
    KT = sb("KT", [128, 4 * TS + 128], BF16)
    VX = sb("VX", [128, 4 * TS // 128 + 1, 130], BF16)
    wout_p = sb("wout_p", [128, 4, D], BF16)
    wout_a = sb("wout_a", [64, 8, D], BF16)
    wpool = sb("wpool", [128, 4, 128], BF16)
    w1f = [sb(f"w1r{i}", [128, 2048], BF16) for i in range(2)]
    w2f = [sb(f"w2r{i}", [128, 2048], BF16) for i in range(4)]
    w1r = [t_[:].rearrange("p (k f) -> p k f", f=256) for t_ in w1f]
    w2r = [t_[:].rearrange("p (j d) -> p j d", d=D) for t_ in w2f]
    WIN_SLOT = [(w1f[0], "w1r0", 0), (w1f[0], "w1r0", 1), (w1f[1], "w1r1", 0), (w1f[1], "w1r1", 1),
                (w2f[0], "w2r0", 0), (w2f[0], "w2r0", 1), (w2f[1], "w2r1", 0), (w2f[1], "w2r1", 1),
                (w2f[2], "w2r2", 0), (w2f[2], "w2r2", 1)]
    hT = sb("hT", [128, 8, 512], F32)
    qTb = [sb(f"qTb{i}", [128, 4, 512], BF16) for i in range(2)]
    ublk = sb("ublk", [128, 4, 528], F32)
    ptmp = [sb(f"ptmp{i}", [128, 528], F32) for i in range(2)]
    dT = sb("dT", [128, 4, 512], BF16)
    pyT = sb("pyT", [128, 4, 512], BF16)
    PT = [sb(f"PT{i}", [128, 2, 512], BF16) for i in range(3)]
    attnT = sb("attnT", [64, 8, 512], BF16)
    ocp = sb("ocp", [128, 2, 512], F32)
    bcs = sb("bcs", [64, 2, 512], F32)
    nT = sb("nT", [128, 8, 512], BF16)
    scr8 = sb("scr8", [128, 8, 512], BF16)
    rstd = sb("rstd", [128, 512], F32)
    ustgs = [sb(f"ustg{i}", [128, 512], F32) for i in range(2)]
    qstg = [sb(f"qstg{i}", [128, 512], BF16) for i in range(2)]
    vstg = sb("vstg", [128, 4, 130], BF16)
    qsqs = [sb(f"qsq{i}", [128, 512], BF16) for i in range(2)]
    qys = [sb(f"qy{i}", [128, 512], F32) for i in range(2)]
    qt = [sb(f"qt{i}", [128, 512], F32) for i in range(2)]
    ropeC = sb("ropeC", [128, 512], F32)
    ropeS = sb("ropeS", [128, 512], F32)
    xstgs = [sb(f"xstg{i}", [128, 1, D], F32) for i in range(2)]
    EG = sb("EG", [128, 4, 4, 16], F32)
    UM = sb("UM", [128, 4, 16], F32)
    HL = sb("HL", [128, 4, 8], F32)
    HR = sb("HR", [128, 4, 8], F32)
    ps = es.enter_context(nc.psum_tensor("ps", [128, 8, 512], F32))

    def P_n1(l, kc): return par[:, l * 8 + kc: l * 8 + kc + 1]
    def P_n2(l, kc): return par[:, 16 + l * 8 + kc: 16 + l * 8 + kc + 1]
    def P_ps(l, g): return par[:, 32 + l * 4 + g: 32 + l * 4 + g + 1]
    def P_qg(l): return par[:, 40 + l: 41 + l]
    def P_kg(l): return par[:, 42 + l: 43 + l]
    MOFF = {"p": 44, "s": 44 + 9}
    def P_mL(n, r): return par[:, MOFF[n] + r: MOFF[n] + r + 1]
    def P_mR(n, R, r): return par[:, MOFF[n] + R + r: MOFF[n] + R + r + 1]
    P_eps = par[:, 63:64]

    def mm(out, lhsT, rhs, start, stop, reads, writes):
        return S.add("pe", lambda e: e.matmul(out, lhsT, rhs, start=start, stop=stop), reads, writes)

    def tr(out, in_, idn, reads, writes):
        return S.add("pe", lambda e: e.transpose(out, in_, idn), reads, writes)

    def act(out, in_, func, reads, writes, scale=None, bias=None):
        kw = {}
        if scale is not None:
            kw["scale"] = scale
        if bias is not None:
            kw["bias"] = bias
        return S.add("act", lambda e: e.activation(out, in_, func, **kw), reads, writes)

    def tt(eng, out, in0, in1, op, reads, writes):
        return S.add(eng, lambda e: e.tensor_tensor(out, in0, in1, op), reads, writes)

    def ts(eng, out, in0, s1, op0, reads, writes):
        return S.add(eng, lambda e: e.tensor_scalar(out, in0, s1, None, op0), reads, writes)

    def stt(out, in0, scalar, in1, op0, op1, reads, writes):
        return S.add("dve", lambda e: e.scalar_tensor_tensor(out, in0, scalar, in1, op0, op1), reads, writes)

    def cp(eng, out, in_, reads, writes):
        return S.add(eng, lambda e: e.tensor_copy(out, in_), reads, writes)

    def mset(eng, ap, val, writes):
        return S.add(eng, lambda e: e.memset(ap, val), (), writes)

    def dma(q, out, in_, reads, writes, key, pwrites=(), **kw):
        return S.add(q, lambda e: e.dma_start(out=out, in_=in_, **kw), reads, writes, dma_key=key, pwrites=pwrites)

    dma("sp", ident[:], ident_in, (), ("ident",), "c_id")
    dma("sp", rt_sb[:], rt_in, (), ("rt",), "c_rt")
    dma("sp", cbf[:], cb_in, (), ("cbf",), "c_cb")
    dma("sp", par[:], par_in, (), ("par",), "c_par")
    dma("sp", icnt[:], icnt_in.rearrange("s p g c -> p s g c"), (), ("icnt",), "c_ic")
    mset("dve", ones_f[:], 1.0, ("ones_f",))
    mset("dve", vstg[:], 1.0, ("vstg",))
    mset("dve", nT[:], 0.0, tuple(f"nT{kc}" for kc in range(8)))

    cast_q = []

    def cast_rows(dst, src, rows, key, wname):
        ncol = src.shape[-1]
        step = 2048 if ncol >= 2048 else ncol
        for r0 in range(0, rows, 256):
            r1 = min(rows, r0 + 256)
            for c0 in range(0, ncol, step):
                cast_q.append(lambda r0=r0, r1=r1, c0=c0: dma(
                    "pool", dst[r0:r1, c0:c0 + step], src[r0:r1, c0:c0 + step], (), (), key, pwrites=(wname,)))

    def cast_win(l):
        cast_rows(w_in_bc[l], w_in_f[l], D, f"cw_in{l}", f"w_in_bc{l}")
        for kc in range(8):
            cast_q.append(lambda kc=kc: dma(
                "pool", w_in_b[l, :, :, kc, :], w_in_bc[l, kc * 128:(kc + 1) * 128, :].rearrange("p (j c) -> j p c", c=128),
                (f"w_in_bc{l}",), (), f"rl_in{l}", pwrites=(f"w_in_b{l}",)))

    def cast_rest(l):
        cast_rows(wp_b[l].rearrange("g c e -> (g c) e"), wp_f[l].rearrange("g c e -> (g c) e"), 512, f"cw_p{l}", f"wp_b{l}")
        cast_rows(w_out_b[l], w_out_f[l], D, f"cw_out{l}", f"w_out_b{l}")
        cast_rows(w1_b[l], w1_f[l], D, f"cw1{l}", f"w1_b{l}")
        cast_rows(w2_b[l], w2_f[l], DFF, f"cw2{l}", f"w2_b{l}")

    def pump(k):
        for _ in range(min(k, len(cast_q))):
            cast_q.pop(0)()

    cast_win(0)
    pump(len(cast_q))
    cast_rest(0)
    cast_win(1)
    n_l0 = len(cast_q)
    cast_rest(1)
    n_l1 = len(cast_q) - n_l0

    def load_win_all(l):
        views = []
        for j in range(10):
            t_, reg, h = WIN_SLOT[j]
            v_ = t_[:, h * 1024:(h + 1) * 1024].rearrange("p (k c) -> p k c", c=128)
            dma("sp", v_, w_in_b[l, j], (f"w_in_b{l}",), (), reg, pwrites=(reg,))
            views.append((v_, reg))
        return views

    def load_wout(l):
        dma("sp", wout_p[:], w_out_b[l, 0:512, :].rearrange("(c p) d -> p c d", p=128), (f"w_out_b{l}",), ("wout_p",), "wout_p")
        dma("sp", wout_a[:], w_out_b[l, 512:1024, :].rearrange("(h p) d -> p h d", p=64), (f"w_out_b{l}",), ("wout_a",), "wout_a")
        dma("sp", wpool[:], wp_b[l].rearrange("g c e -> c g e"), (f"wp_b{l}",), ("wpool",), "wpool")

    def rstd_from_ms(out, in_ps, reads, wname):
        act(out, in_ps, AF.Ln, reads + ("par",), (wname,), bias=P_eps)
        act(out, out, AF.Exp, (wname,), (wname,), scale=-0.5)

    def rmsnorm_big(nt, gfun, bank):
        for kc in range(8):
            eng = ("act", "dve", "pool", "act", "dve", "pool", "act", "dve")[kc]
            if eng == "act":
                act(scr8[:, kc, :nt], hT[:, kc, :nt], AF.Square, (f"hT{kc}",), (f"scr8_{kc}",))
            else:
                tt(eng, scr8[:, kc, :nt], hT[:, kc, :nt], hT[:, kc, :nt], ALU.mult, (f"hT{kc}",), (f"scr8_{kc}",))
        for kc in range(8):
            mm(ps[:, bank, :nt], onesD, scr8[:, kc, :nt], kc == 0, kc == 7, (f"scr8_{kc}", "cbf"), (f"ps{bank}",))
        rstd_from_ms(rstd[:, :nt], ps[:, bank, :nt], (f"ps{bank}",), "rstd")
        for kc in range(8):
            stt(nT[:, kc, :nt], hT[:, kc, :nt], gfun(kc), rstd[:, :nt], ALU.mult, ALU.mult,
                (f"hT{kc}", "rstd", "par"), (f"nT{kc}",))

    def a_part(l, n, blk, nt):
        k = (l, n)
        si = 0 if n == "p" else 1
        c0 = 0 if blk < 0 else NMETA + blk * 512
        dma("sp", ropeC[:, :nt], rope_in[si, 0, :, c0:c0 + nt], (), ("ropeC",), "ropeC")
        dma("sp", ropeS[:, :nt], rope_in[si, 1, :, c0:c0 + nt], (), ("ropeS",), "ropeS")
        import os as _o
        KA = int(_o.environ.get("KA", "9"))
        wviews = load_win_all(l)
        dma("sp", hT_d[k][:, :, c0:c0 + nt].rearrange("kc p t -> p kc t"), hT[:, :, :nt],
            tuple(f"hT{kc}" for kc in range(8)), (f"hTd{l}{n}b{blk}",), "hst")
        rmsnorm_big(nt, lambda kc: P_n1(l, kc), 7)
        if KA < 2:
            return
        pb = [0]

        def nextbank():
            b_ = pb[0] % 4
            pb[0] += 1
            return b_

        ubank = [nextbank() for _ in range(4)]
        for kc in range(8):
            for j in range(2):
                wt, wreg = wviews[j]
                mm(ps[:, ubank[j], :nt], wt[:, kc, :], nT[:, kc, :nt], kc == 0, kc == 7, (wreg, f"nT{kc}"), (f"ps{ubank[j]}",))
        for j in range(4):
            b = ubank[j]
            wt, wreg = wviews[j]
            if j >= 2:
                for kc in range(8):
                    mm(ps[:, b, :nt], wt[:, kc, :], nT[:, kc, :nt], kc == 0, kc == 7, (wreg, f"nT{kc}"), (f"ps{b}",))
            ustg = ustgs[j % 2]
            ur = f"ustg{j % 2}"
            act(ustg[:, :nt], ps[:, b, :nt], AF.Copy, (f"ps{b}",), (ur,))
            if blk < 0:
                dma("sp", umT_d[k][j], ustg[:, :16], (ur,), (), f"ust{j % 2}", pwrites=(f"umT{l}{n}",))
            else:
                dma("sp", uT_d[k][j, :, blk * 512:(blk + 1) * 512], ustg[:, :512], (ur,), (), f"ust{j % 2}", pwrites=(f"uT{l}{n}b{blk}",))
                if blk == 0:
                    dma("sp", e_send[k].ap()[j * 128:(j + 1) * 128, 0:8], ustg[:, 0:8], (ur,), (), f"est{j % 2}", pwrites=(f"es{l}{n}",))
                if blk == NB - 1:
                    dma("sp", e_send[k].ap()[j * 128:(j + 1) * 128, 8:16], ustg[:, 504:512], (ur,), (), f"est{j % 2}", pwrites=(f"es{l}{n}",))
        if KA < 3:
            return
        def stage2(j):
            p2 = j % 2
            qsq, qy = qsqs[p2], qys[p2]
            bm, br = 4 + p2, 6 + p2
            mm(ps[:, bm, :nt], blk64, qsq[:, :nt], True, True, (f"qsq{p2}", "cbf"), (f"ps{bm}",))
            mm(ps[:, br, :nt], rt_sb[:], qy[:, :nt], True, True, (f"qy{p2}", "rt"), (f"ps{br}",))
            rstd_from_ms(rstd[:, :nt], ps[:, bm, :nt], (f"ps{bm}",), "rstd")
            tt("pool", qt[0][:, :nt], qy[:, :nt], ropeC[:, :nt], ALU.mult, (f"qy{p2}", "ropeC"), ("qt0",))
            tt("dve", qt[1][:, :nt], ps[:, br, :nt], ropeS[:, :nt], ALU.mult, (f"ps{br}", "ropeS"), ("qt1",))
            tt("pool", qt[0][:, :nt], qt[0][:, :nt], qt[1][:, :nt], ALU.add, ("qt0", "qt1"), ("qt0",))
            st = qstg[p2]
            sreg = f"qstg{p2}"
            tt("dve", st[:, :nt], qt[0][:, :nt], rstd[:, :nt], ALU.mult, ("qt0", "rstd"), (sreg,))
            if j < 4:
                dma("sp", qT_d[k][j, :, c0:c0 + nt], st[:, :nt], (sreg,), (), f"qst{p2}", pwrites=(f"qT{l}{n}b{blk}",))
            elif blk < 0:
                dma("sp", kT_meta[k], st[:, :16], (sreg,), (f"kTm{l}{n}",), f"qst{p2}")
            else:
                dma("sp", kT_send[k].ap()[:, blk * 512:(blk + 1) * 512], st[:, :512], (sreg,), (), f"qst{p2}", pwrites=(f"kTs{l}{n}",))

        for j in range(5):
            b = nextbank()
            p2 = j % 2
            wt, wreg = wviews[4 + j]
            for kc in range(8):
                mm(ps[:, b, :nt], wt[:, kc, :], nT[:, kc, :nt], kc == 0, kc == 7, (wreg, f"nT{kc}"), (f"ps{b}",))
            g = P_qg(l) if j < 4 else P_kg(l)
            act(qsqs[p2][:, :nt], ps[:, b, :nt], AF.Square, (f"ps{b}",), (f"qsq{p2}",))
            ts("dve", qys[p2][:, :nt], ps[:, b, :nt], g, ALU.mult, (f"ps{b}", "par"), (f"qy{p2}",))
            if j > 0:
                stage2(j - 1)
        stage2(4)
        if KA < 4:
            return
        wt, wreg = wviews[9]
        ntile = max(1, nt // 128)
        rows = min(128, nt)
        for t in range(ntile):
            b = nextbank()
            for kc in range(8):
                mm(ps[:, b, 0:128], nT[:, kc, t * 128:t * 128 + 128], wt[:, kc, :], kc == 0, kc == 7,
                   (wreg, f"nT{kc}"), (f"ps{b}",))
            S.add("act", lambda e, t=t, b=b: e.activation(
                vstg[:rows, t, :].rearrange("p (g c) -> p g c", g=2)[:, :, 0:64],
                ps[:rows, b, 0:128].rearrange("p (g c) -> p g c", g=2), AF.Copy), (f"ps{b}", "vstg"), ("vstg",))
        if blk < 0:
            dma("sp", v_meta[k], vstg[:16, 0, :], ("vstg",), (f"vm{l}{n}",), "vst")
        else:
            dma("sp", v_send[k].ap()[blk * 512:(blk + 1) * 512, :].rearrange("(t p) c -> p t c", p=128), vstg[:],
                ("vstg",), (), "vst", pwrites=(f"vs{l}{n}",))

    def exchange(l, n, R):
        k = (l, n)
        groups = [list(range(i, i + R)) for i in range(0, NCORES, R)]
        for (src, dst, rname, wname) in (
            (kT_send[k], kT_gath[k], f"kTs{l}{n}", f"kTg{l}{n}"),
            (v_send[k], v_gath[k], f"vs{l}{n}", f"vg{l}{n}"),
            (e_send[k], e_gath[k], f"es{l}{n}", f"eg{l}{n}"),
        ):
            S.add("pool", lambda e, src=src, dst=dst: e.collective_compute(
                "AllGather", ALU.bypass, replica_groups=groups,
                ins=[src.ap().opt()], outs=[dst.ap().opt()]), (rname,), (wname,), dma_key=f"cc_{wname}", inc=1)

    def load_kv(l, n, R):
        k = (l, n)
        for r in range(R):
            dma("sp", KT[:, r * TS:(r + 1) * TS], kT_gath[k].ap()[r * 128:(r + 1) * 128, :], (f"kTg{l}{n}",), (), "kvl_k", pwrites=("KT",))
        S.add("dve", lambda e: e.memset(KT[:, R * TS:R * TS + 128], 0.0), (), ("KTm",), pwrites=("KT",))
        dma("sp", KT[:, R * TS:R * TS + 16], kT_meta[k], (f"kTm{l}{n}",), ("KTm",), "kvl_km")
        for r in range(R):
            dma("sp", VX[:, r * (TS // 128):(r + 1) * (TS // 128), :],
                v_gath[k].ap()[r * TS:(r + 1) * TS, :].rearrange("(i p) c -> p i c", p=128), (f"vg{l}{n}",), (), "kvl_v", pwrites=("VX",))
        S.add("dve", lambda e: e.memset(VX[:, R * TS // 128, :], 0.0), (), ("VXm",), pwrites=("VX",))
        dma("sp", VX[:16, R * TS // 128, :], v_meta[k], (f"vm{l}{n}",), ("VXm",), "kvl_vm")
        dma("sp", EG[:, :R, :, :], e_gath[k].ap().rearrange("(r g p) c -> p r g c", g=4, p=128), (f"eg{l}{n}",), ("EG",), "kvl_e")
        dma("sp", UM[:], umT_d[k].rearrange("g p c -> p g c"), (f"umT{l}{n}",), ("UM",), "kvl_u")
        ts("dve", HL[:], UM[:, :, 8:16], P_mL(n, 0), ALU.mult, ("UM", "par"), ("HL",))
        for r in range(1, R):
            stt(HL[:], EG[:, r - 1, :, 8:16], P_mL(n, r), HL[:], ALU.mult, ALU.add, ("EG", "par", "HL"), ("HL",))
        ts("dve", HR[:], EG[:, 1, :, 0:8], P_mR(n, R, 0), ALU.mult, ("EG", "par"), ("HR",))
        for r in range(1, R - 1):
            stt(HR[:], EG[:, r + 1, :, 0:8], P_mR(n, R, r), HR[:], ALU.mult, ALU.add, ("EG", "par", "HR"), ("HR",))

    def b_loads(l, n, blk, nt, qi):
        k = (l, n)
        c0 = 0 if blk < 0 else NMETA + blk * 512
        dma("sp", qTb[qi][:, :, :nt], qT_d[k][:, :, c0:c0 + nt].rearrange("c p t -> p c t"), (f"qT{l}{n}b{blk}",), (f"qTb{qi}",), f"qld{qi}")

    def pool_mix(l, n, R, blk, nt):
        k = (l, n)
        si = 0 if n == "p" else 1
        W = nt + 16
        if blk < 0:
            mset("dve", ublk[:, :, 0:8], 0.0, ("ublk",))
            cp("dve", ublk[:, :, 8:24], UM[:], ("UM", "ublk"), ("ublk",))
            cp("dve", ublk[:, :, 24:32], EG[:, 0, :, 0:8], ("EG", "ublk"), ("ublk",))
        else:
            lo = max(blk * 512 - 8, 0)
            hi = min(blk * 512 + 520, TS)
            o = lo - (blk * 512 - 8)
            dma("sp", ublk[:, :, o:o + hi - lo], uT_d[k][:, :, lo:hi].rearrange("g p t -> p g t"),
                tuple(f"uT{l}{n}b{bb}" for bb in range(NB)), ("ublk",), "uld")
            if blk == 0:
                cp("dve", ublk[:, :, 0:8], HL[:], ("HL", "ublk"), ("ublk",))
            if blk == NB - 1:
                cp("dve", ublk[:, :, 520:528], HR[:], ("HR", "ublk"), ("ublk",))
        for g, w in enumerate((2, 4, 8, 16)):
            hw = w // 2
            src = ublk[:, g, :]
            ln = W
            sh = 1
            ti = 0
            sname = "ublk"
            while sh < w:
                ln2 = ln - sh
                dst = ptmp[ti]
                tt("dve", dst[:, :ln2], src[:, 0:ln2], src[:, sh:sh + ln2], ALU.add, (sname,), (f"ptmp{ti}",))
                src, sname, ln = dst, f"ptmp{ti}", ln2
                ti ^= 1
                sh *= 2
            off = 8 - hw
            if blk < 0:
                tab = icnt[:, si, g, 0:16]
                o2 = ptmp[ti]
                tt("dve", o2[:, :nt], src[:, off:off + nt], tab, ALU.mult, (sname, "icnt"), (f"ptmp{ti}",))
                tt("dve", dT[:, g, :nt], o2[:, :nt], ublk[:, g, 8:8 + nt], ALU.subtract, (f"ptmp{ti}", "ublk"), (f"dT{g}",))
            else:
                n1 = nt - 16 if blk == NB - 1 else nt
                stt(dT[:, g, :n1], src[:, off:off + n1], 1.0 / w, ublk[:, g, 8:8 + n1], ALU.mult, ALU.subtract,
                    (sname, "ublk"), (f"dT{g}",))
                if blk == NB - 1:
                    tab = icnt[:, si, g, 16:32]
                    o2 = ptmp[ti]
                    tt("dve", o2[:, :16], src[:, off + n1:off + nt], tab, ALU.mult, (sname, "icnt"), (f"ptmp{ti}",))
                    tt("dve", dT[:, g, n1:nt], o2[:, :16], ublk[:, g, 8 + n1:8 + nt], ALU.subtract,
                       (f"ptmp{ti}", "ublk", f"dT{g}"), (f"dT{g}",))

    def pool_pe(l, nt):
        for g in range(4):
            b = g % 2
            mm(ps[:, b, :nt], wpool[:, g, :], dT[:, g, :nt], True, True, ("wpool", f"dT{g}"), (f"ps{b}",))
            act(pyT[:, g, :nt], ps[:, b, :nt], AF.Copy, (f"ps{b}", "par"), (f"pyT{g}",), scale=P_ps(l, g))

    def attention(l, n, R, nt, qi):
        nfull = R * TS // 128
        ntile = nfull + 1
        q = qTb[qi]
        pending = []
        for c in range(4):
            heads = (c, c + 4)

            def qk(i, c=c):
                slot = i % 3
                for hh in range(2):
                    b = slot * 2 + hh
                    mm(ps[:, b, :nt], KT[64 * hh:64 * hh + 64, i * 128:i * 128 + 128], q[64 * hh:64 * hh + 64, c, :nt],
                       True, True, ("KT", "KTm", f"qTb{qi}"), (f"ps{b}",))

            qk(0)
            qk(1)
            for i in range(ntile):
                slot = i % 3
                if i + 2 < ntile:
                    qk(i + 2)
                act(PT[slot][:, :, :nt], ps[:, slot * 2:slot * 2 + 2, :nt], AF.Exp,
                    (f"ps{slot * 2}", f"ps{slot * 2 + 1}"), (f"PT{slot}",), scale=0.125)
                if i == 1 and pending:
                    pending.pop()()
                for hh in range(2):
                    mm(ps[:65, 6 + hh, :nt], VX[:, i, hh * 65:hh * 65 + 65], PT[slot][:, hh, :nt],
                       i == 0, i == ntile - 1, ("VX", "VXm", f"PT{slot}"), (f"ps{6 + hh}",))
            for hh in range(2):
                cp("dve", ocp[:65, hh, :nt], ps[:65, 6 + hh, :nt], (f"ps{6 + hh}",), (f"ocp{hh}", f"ocpd{hh}"))

            def epilogue(heads=heads, c=c):
                if c < 3:
                    S.add("dve", lambda e: e.reciprocal(ocp[64:65, :, :nt], ocp[64:65, :, :nt]),
                          ("ocpd0", "ocpd1"), ("ocpd0", "ocpd1"))
                else:
                    act(ocp[64:65, :, :nt], ocp[64:65, :, :nt], AF.Ln, ("ocpd0", "ocpd1"), ("ocpd0", "ocpd1"))
                    act(ocp[64:65, :, :nt], ocp[64:65, :, :nt], AF.Exp, ("ocpd0", "ocpd1"), ("ocpd0", "ocpd1"), scale=-1.0)
                dma("sp", rec_d[:, :nt], ocp[64:65, :, :nt], ("ocpd0", "ocpd1"), ("recd",), "bcd0")
                dma("sp", bcs[:, :, :nt], rec_d[:, :nt].partition_broadcast(64), ("recd",), ("bcs",), "bcd1")
                for hh in range(2):
                    tt("dve", attnT[:, heads[hh], :nt], ocp[:64, hh, :nt], bcs[:, hh, :nt], ALU.mult,
                       (f"ocp{hh}", "bcs"), (f"attnT{heads[hh]}",))

            pending.append(epilogue)
        pending.pop()()

    def out_proj(nt):
        for j in range(8):
            b = j % 2
            for i in range(4):
                mm(ps[:, b, :nt], wout_p[:, i, j * 128:(j + 1) * 128], pyT[:, i, :nt], i == 0, False,
                   ("wout_p", f"pyT{i}"), (f"ps{b}",))
            horder = (0, 4, 1, 5, 2, 6, 3, 7)
            for hi, h in enumerate(horder):
                mm(ps[:, b, :nt], wout_a[:, h, j * 128:(j + 1) * 128], attnT[:, h, :nt], False, hi == 7,
                   ("wout_a", f"attnT{h}"), (f"ps{b}",))
            tt("dve", hT[:, j, :nt], ps[:, b, :nt], hT[:, j, :nt], ALU.add, (f"ps{b}", f"hT{j}"), (f"hT{j}",))

    wctr = [0, 0]

    def mlp(l, nt):
        w1slot, w2slot = {}, {}

        def issue_w1(gi):
            if gi in w1slot or gi > 15:
                return
            s1 = wctr[0] % 2
            wctr[0] += 1
            dma("sp", w1r[s1], w1_b[l][:, gi * 256:(gi + 1) * 256].rearrange("(kc p) f -> p kc f", p=128),
                (f"w1_b{l}",), (f"w1r{s1}",), f"w1r{s1}")
            w1slot[gi] = s1

        def issue_w2(gi):
            if gi in w2slot or gi > 15:
                return
            s2 = wctr[1] % 4
            wctr[1] += 1
            dma("sp", w2r[s2], w2_b[l][gi * 256:(gi + 1) * 256, :].rearrange("(j p) d -> p j d", p=128),
                (f"w2_b{l}",), (f"w2r{s2}",), f"w2r{s2}")
            w2slot[gi] = s2

        issue_w1(0)
        issue_w1(1)
        rmsnorm_big(nt, lambda kc: P_n2(l, kc), 7)
        for qtr in range(4):
            for fg in range(4):
                gi = qtr * 4 + fg
                issue_w1(gi)
                issue_w1(gi + 1)
                if fg == 2:
                    issue_w2(qtr * 4)
                    issue_w2(qtr * 4 + 1)
                s1 = w1slot[gi]
                if qtr == 0 and fg == 0:
                    for kc in range(8):
                        for ff in range(2):
                            mm(ps[:, 2 + ff, :nt], w1r[s1][:, kc, ff * 128:(ff + 1) * 128], nT[:, kc, :nt], kc == 0, kc == 7,
                               (f"w1r{s1}", f"nT{kc}"), (f"ps{2 + ff}",))
                for ff in range(2):
                    fc = fg * 2 + ff
                    b = 2 + (fc % 2)
                    if not (qtr == 0 and fg == 0):
                        for kc in range(8):
                            mm(ps[:, b, :nt], w1r[s1][:, kc, ff * 128:(ff + 1) * 128], nT[:, kc, :nt], kc == 0, kc == 7,
                               (f"w1r{s1}", f"nT{kc}"), (f"ps{b}",))
                    rb = qt[fc % 2]
                    act(rb[:, :nt], ps[:, b, :nt], AF.Relu, (f"ps{b}",), (f"qt{fc % 2}",))
                    tt("pool", scr8[:, fc, :nt], rb[:, :nt], rb[:, :nt], ALU.mult, (f"qt{fc % 2}",), (f"scr8_{fc}",))
            for half in range(2):
                g0 = qtr * 4 + half * 2
                issue_w2(g0)
                issue_w2(g0 + 1)
                issue_w2(g0 + 2)
                issue_w2(g0 + 3)
                sl2 = [w2slot[g0], w2slot[g0 + 1]]
                for j in range(8):
                    b = j % 2
                    for fg in range(2):
                        for ff in range(2):
                            fc = half * 4 + fg * 2 + ff
                            mm(ps[:, b, :nt], w2r[sl2[fg]][:, ff, j * 128:(j + 1) * 128], scr8[:, fc, :nt],
                               fg == 0 and ff == 0, fg == 1 and ff == 1, (f"w2r{sl2[fg]}", f"scr8_{fc}"), (f"ps{b}",))
                    tt("dve", hT[:, j, :nt], ps[:, b, :nt], hT[:, j, :nt], ALU.add, (f"ps{b}", f"hT{j}"), (f"hT{j}",))

    x_issued = set()

    def x_dma(n, blk, q4):
        if blk < 0 or (n, blk, q4) in x_issued:
            return
        x_issued.add((n, blk, q4))
        r0 = blk * 512 + q4 * 128
        dma("sp", xstgs[q4 % 2][:, 0, :], x_in[n][r0:r0 + 128, :], (), (f"xstg{q4 % 2}",), f"xld{q4 % 2}")

    def load_x(n, blk, nt):
        if blk < 0:
            dma("sp", hT[:, :, 0:16], meta_in.rearrange("(kc p) t -> p kc t", p=128), (),
                tuple(f"hT{kc}" for kc in range(8)), "hld")
            return
        x_dma(n, blk, 0)
        x_dma(n, blk, 1)
        for q4 in range(4):
            xstg = xstgs[q4 % 2]
            xr = f"xstg{q4 % 2}"
            for kc in range(8):
                b = kc % 2
                tr(ps[:, b, 0:128], xstg[:, 0, kc * 128:(kc + 1) * 128], ident[:], (xr, "ident"), (f"ps{b}",))
                act(hT[:, kc, q4 * 128:(q4 + 1) * 128], ps[:, b, 0:128], AF.Copy, (f"ps{b}", f"hT{kc}"), (f"hT{kc}",))
            if q4 + 2 < 4:
                x_dma(n, blk, q4 + 2)

    def store_y(n, blk, nt):
        for q4 in range(4):
            tok = q4 * 128
            xstg = xstgs[q4 % 2]
            xr = f"xstg{q4 % 2}"
            for kc in range(8):
                b = 2 + (kc // 4) % 2
                tr(ps[:, b, (kc % 4) * 128:(kc % 4) * 128 + 128], hT[:, kc, tok:tok + 128], ident[:],
                   (f"hT{kc}", "ident"), (f"ps{b}",))
                if kc % 4 == 3:
                    act(xstg[:, 0, (kc // 4) * 512:(kc // 4) * 512 + 512], ps[:, b, :], AF.Copy, (f"ps{b}", xr), (xr,))
            r0 = blk * 512 + tok
            dma("sp", y_out[n][r0:r0 + 128, :], xstg[:, 0, :], (xr,), (), f"yst{q4 % 2}", pwrites=(f"y{n}",))

    blocks = [(-1, NMETA)] + [(b, 512) for b in range(NB)]
    import os as _os
    KSTOP = int(_os.environ.get("KSTOP", "9"))
    for sl in SL:
        n, R = sl["n"], sl["R"]
        for bi0, (blk, nt) in enumerate(blocks):
            if KSTOP >= 1 and not (blk < 0 and _os.environ.get("KNOMETA")):
                load_x(n, blk, nt)
                if bi0 + 1 < len(blocks):
                    x_dma(n, blocks[bi0 + 1][0], 0)
                    x_dma(n, blocks[bi0 + 1][0], 1)
            if KSTOP >= 2 and not (blk < 0 and _os.environ.get("KNOMETA")):
                a_part(0, n, blk, nt)
            pump(6 if blk >= 0 else 2)
        if KSTOP >= 3:
            exchange(0, n, R)
    pump(max(0, len(cast_q) - n_l1))
    qi = 0
    order = [(l, sl["n"], sl["R"]) for l in range(2 if KSTOP >= 4 else 0) for sl in SL]
    for oi, (l, n, R) in enumerate(order):
        if n == "p":
            load_wout(l)
        if True:
            if oi == 0:
                load_kv(l, n, R)
            blks = blocks if l == 0 else blocks[1:]
            b_loads(l, n, blks[0][0], blks[0][1], qi)
            for bi, (blk, nt) in enumerate(blks):
                c0 = 0 if blk < 0 else NMETA + blk * 512
                cur = qi
                if bi + 1 < len(blks):
                    b_loads(l, n, blks[bi + 1][0], blks[bi + 1][1], 1 - qi)
                dma("sp", hT[:, :, :nt], hT_d[(l, n)][:, :, c0:c0 + nt].rearrange("kc p t -> p kc t"),
                    (f"hTd{l}{n}b{blk}",), tuple(f"hT{kc}" for kc in range(8)), "hld")
                pool_mix(l, n, R, blk, nt)
                attention(l, n, R, nt, cur)
                if bi == len(blks) - 1 and oi + 1 < len(order):
                    load_kv(*order[oi + 1])
                pool_pe(l, nt)
                out_proj(nt)
                mlp(l, nt)
                if l == 0:
                    a_part(1, n, blk, nt)
                    pump(3)
                elif blk >= 0:
                    store_y(n, blk, nt)
                qi = 1 - qi
            if l == 0:
                exchange(1, n, R)
                if n == "s":
                    pump(len(cast_q))

    names = S.finalize()
    S.final = [("dma:" + k_, (1 if k_.startswith("cc_") else 16) * v_) for k_, v_ in S.dma_cnt.items()]
    print("kernel build: sbuf bytes/partition", sbtot[0])
    print("kernel build: ops", {e: len(S.ops[e]) for e in ENGS}, "sems", len(names))
    sems = {nm: es.enter_context(nc.semaphore(nm.replace(":", "_"))) for nm in names}
    with nc.Block() as block:
        @block.tensor
        def _(e):
            S.emit("pe", e, sems)

        @block.scalar
        def _(e):
            S.emit("act", e, sems)

        @block.vector
        def _(e):
            S.emit("dve", e, sems)

        @block.gpsimd
        def _(e):
            S.emit("pool", e, sems)

        @block.sync
        def _(e):
            S.emit("sp", e, sems)
    es.close()
    return nc


def _rope_tables(n_tok_total, tok0, ntok):
    inv = (10000.0 ** (-np.arange(16, dtype=np.float32) / 16)).astype(np.float32)
    t = np.arange(tok0, tok0 + ntok)
    r = np.concatenate([np.zeros(NMETA), t // GRID_W]).astype(np.float32)
    c = np.concatenate([np.zeros(NMETA), t % GRID_W]).astype(np.float32)
    p = np.arange(128)
    j = p % 64
    axis = j // 32
    fi = j % 16
    pos = np.where(axis[:, None] == 0, r[None, :], c[None, :]).astype(np.float32)
    ang = (pos * inv[fi][:, None]).astype(np.float32)
    return np.cos(ang).astype(np.float32), np.sin(ang).astype(np.float32)


def _consts():
    ident = np.eye(128, dtype=np.float32)
    R = np.zeros((128, 128), np.float32)
    for m in range(128):
        if (m % 32) < 16:
            R[m, m + 16] = -1.0
        else:
            R[m, m - 16] = 1.0
    rt = np.ascontiguousarray(R.T)
    cb = np.zeros((128, 384), np.float32)
    cb[:, 0:128] = 1.0 / D
    for h in range(2):
        cb[64 * h:64 * h + 64, 128 + 64 * h:128 + 64 * h + 64] = 1.0 / 64
    cb[:, 256:384] = 1.0
    return ident, rt, cb.astype(ml_dtypes.bfloat16)


_NC_CACHE = {}


def _run(inputs, TS):
    f = lambda a: np.ascontiguousarray(np.asarray(a, dtype=np.float32))
    xp, xs = f(inputs["x_prompt"]), f(inputs["x_sample"])
    w_in = f(inputs["w_in"])
    perm = list(range(512))
    for c in range(4):
        perm += list(range(512 + c * 64, 512 + c * 64 + 64)) + list(range(512 + (c + 4) * 64, 512 + (c + 4) * 64 + 64))
    perm += list(range(1024, 1280))
    w_in = np.ascontiguousarray(w_in[:, :, perm])
    n1, n2 = f(inputs["norm1_g"]), f(inputs["norm2_g"])
    psc = f(inputs["pool_scale"])
    qg, kg = f(inputs["q_norm_g"]), f(inputs["k_norm_g"])
    ident, rt, cb = _consts()
    Lp, Ls = 4 * TS + NMETA, 2 * TS + NMETA
    in_maps = []
    for c in range(NCORES):
        pb, pq, sbi, sh = c // 4, c % 4, c // 2, c % 2
        par = np.zeros((128, 64), np.float32)
        for l in range(2):
            par[:, l * 8:(l + 1) * 8] = n1[l].reshape(8, 128).T
            par[:, 16 + l * 8:16 + (l + 1) * 8] = n2[l].reshape(8, 128).T
            par[:, 32 + l * 4:32 + (l + 1) * 4] = psc[l].reshape(4, 128).T
            par[:, 40 + l] = np.tile(qg[l], 2)
            par[:, 42 + l] = np.tile(kg[l], 2)
        for (off, R, rk) in ((44, 4, pq), (53, 2, sh)):
            par[:, off + rk] = 1.0
            if rk < R - 1:
                par[:, off + R + rk] = 1.0
        rope = np.zeros((2, 2, 128, NMETA + TS), np.float32)
        rope[0, 0], rope[0, 1] = _rope_tables(4 * TS, pq * TS, TS)
        rope[1, 0], rope[1, 1] = _rope_tables(2 * TS, sh * TS, TS)
        icnt = np.zeros((2, 128, 4, 32), np.float32)
        for si, (L, R, rk) in enumerate(((Lp, 4, pq), (Ls, 2, sh))):
            for g, w in enumerate((2, 4, 8, 16)):
                t = np.arange(16)
                lo = np.clip(t - w // 2, 0, L)
                hi = np.clip(t + (w - w // 2), 0, L)
                icnt[si, :, g, 0:16] = (1.0 / (hi - lo).astype(np.float32))[None, :]
                if rk == R - 1:
                    t = np.arange(L - 16, L)
                    lo = np.clip(t - w // 2, 0, L)
                    hi = np.clip(t + (w - w // 2), 0, L)
                    icnt[si, :, g, 16:32] = (1.0 / (hi - lo).astype(np.float32))[None, :]
                else:
                    icnt[si, :, g, 16:32] = 1.0 / w
        par[:, 63] = EPS
        in_maps.append({
            "xp": np.ascontiguousarray(xp[pb, pq * TS:(pq + 1) * TS]),
            "xs": np.ascontiguousarray(xs[sbi, sh * TS:(sh + 1) * TS]),
            "metaT": np.ascontiguousarray(f(inputs["meta_tokens"]).T),
            "w_in": w_in, "w_out": f(inputs["w_out"]), "w1": f(inputs["w_mlp_in"]), "w2": f(inputs["w_mlp_out"]),
            "w_pool": f(inputs["w_pool"]),
            "params": par, "rope": rope, "icnt": icnt, "ident": ident, "rt": rt, "cbf": cb,
        })
    if TS not in _NC_CACHE:
        _NC_CACHE[TS] = build(TS)
    nc = _NC_CACHE[TS]
    res = run_bass_kernel_spmd(nc, in_maps, core_ids=list(range(NCORES)))
    yp = np.zeros((2, 4 * TS, D), np.float32)
    ys = np.zeros((4, 2 * TS, D), np.float32)
    for c in range(NCORES):
        pb, pq, sbi, sh = c // 4, c % 4, c // 2, c % 2
        yp[pb, pq * TS:(pq + 1) * TS] = res.results[c]["yp"]
        ys[sbi, sh * TS:(sh + 1) * TS] = res.results[c]["ys"]
    return yp, ys


def kernel(**inputs):
    TS = inputs["x_prompt"].shape[1] // 4
    return _run(inputs, TS)
```

```python
import numpy as np
import ml_dtypes
from contextlib import ExitStack
import concourse.bass as bass
import concourse.mybir as mybir
from concourse.bass_utils import run_bass_kernel_spmd

F32 = mybir.dt.float32
BF16 = mybir.dt.bfloat16
AF = mybir.ActivationFunctionType
ALU = mybir.AluOpType

D = 1024
DFF = 4096
NMETA = 16
GRID_W = 64
EPS = 1e-6
INW = 1280
NCORES = 8
ENGS = ("pe", "act", "dve", "pool", "sp")


class Op:
    __slots__ = ("eng", "fn", "deps", "is_dma", "key", "needed", "tok", "inc", "pos")


class Sched:
    def __init__(self):
        self.ops = {e: [] for e in ENGS}
        self.reg = {}
        self.dma_cnt = {}
        self.final = []

    def add(self, eng, fn, reads=(), writes=(), dma_key=None, pwrites=(), inc=16):
        op = Op()
        op.eng = eng
        op.fn = fn
        op.is_dma = dma_key is not None
        op.key = dma_key
        op.needed = False
        op.tok = None
        op.inc = inc
        psr = tuple(r for r in reads if r.startswith("ps") and r not in writes)
        reads = tuple(r for r in reads if not r.startswith("ps"))
        writes = tuple(writes) + psr
        deps = []
        for r in reads:
            st = self.reg.setdefault(r, [[], [], []])
            deps += st[0]
        for wname in writes:
            st = self.reg.setdefault(wname, [[], [], []])
            deps += st[0] + st[1] + st[2]
        for wname in pwrites:
            st = self.reg.setdefault(wname, [[], [], []])
            if st[1]:
                st[2] = st[2] + st[0] + st[1]
                st[0] = []
                st[1] = []
            deps += st[2]
        seen = set()
        op.deps = []
        for d in deps:
            if id(d) not in seen and d is not op:
                seen.add(id(d))
                op.deps.append(d)
        for r in reads:
            if r not in writes:
                self.reg[r][1].append(op)
        for wname in writes:
            self.reg[wname] = [[op], [], []]
        for wname in pwrites:
            self.reg[wname][0].append(op)
        if op.is_dma:
            c = self.dma_cnt.get(dma_key, 0) + 1
            self.dma_cnt[dma_key] = c
            op.tok = ("dma:" + dma_key, inc * c)
        self.ops[eng].append(op)
        return op

    def finalize(self):
        for e in ENGS:
            pos = {id(op): i for i, op in enumerate(self.ops[e])}
            for op in self.ops[e]:
                op.pos = pos[id(op)]
        for e in ENGS:
            for op in self.ops[e]:
                last = {}
                for d in op.deps:
                    if d.is_dma:
                        continue
                    if e == "pe" and d.eng == "pe":
                        continue
                    if d.eng not in last or d.pos > last[d.eng].pos:
                        last[d.eng] = d
                for d in last.values():
                    d.needed = True
        for e in ENGS:
            c = 0
            for op in self.ops[e]:
                if not op.is_dma and op.needed:
                    c += 1
                    op.tok = ("eng:" + e, c)
            nxt = None
            for op in reversed(self.ops[e]):
                if op.is_dma:
                    continue
                if op.needed:
                    nxt = op.tok
                elif op.tok is None:
                    op.tok = nxt
        names = ["eng:" + e for e in ENGS] + ["dma:" + k for k in self.dma_cnt]
        return names

    def emit(self, e, handle, sems):
        waited = {}
        for op in self.ops[e]:
            need = {}
            for d in op.deps:
                if e == "pe" and d.eng == "pe" and not d.is_dma:
                    continue
                s, v = d.tok
                if v > need.get(s, 0):
                    need[s] = v
            for s, v in need.items():
                if waited.get(s, 0) >= v:
                    continue
                handle.wait_ge(sems[s], v)
                waited[s] = v
            inst = op.fn(handle)
            if op.is_dma:
                inst.then_inc(sems[op.tok[0]], op.inc)
            elif op.needed:
                inst.then_inc(sems[op.tok[0]], 1)
        if e == "sp":
            for s, v in self.final:
                if waited.get(s, 0) < v:
                    handle.wait_ge(sems[s], v)
                    waited[s] = v


def build(TS):
    NB = TS // 512
    NT = NMETA + TS
    SL = [dict(n="p", R=4), dict(n="s", R=2)]
    nc = bass.Bass("TRN2", target_bir_lowering=False)
    S = Sched()

    def din(name, shape, dt=F32):
        return nc.dram_tensor(name, list(shape), dt, kind="ExternalInput").ap()

    def dout(name, shape, dt=F32):
        return nc.dram_tensor(name, list(shape), dt, kind="ExternalOutput").ap()

    def dint(name, shape, dt):
        return nc.dram_tensor(name, list(shape), dt)

    x_in = {"p": din("xp", [TS, D]), "s": din("xs", [TS, D])}
    meta_in = din("metaT", [D, NMETA])
    y_out = {"p": dout("yp", [TS, D]), "s": dout("ys", [TS, D])}
    w_in_f = din("w_in", [2, D, INW])
    w_out_f = din("w_out", [2, D, D])
    w1_f = din("w1", [2, D, DFF])
    w2_f = din("w2", [2, DFF, D])
    wp_f = din("w_pool", [2, 4, 128, 128])
    NPAR = 64
    par_in = din("params", [128, NPAR])
    rope_in = din("rope", [2, 2, 128, NT])
    icnt_in = din("icnt", [2, 128, 4, 32])
    ident_in = din("ident", [128, 128])
    rt_in = din("rt", [128, 128])
    cb_in = din("cbf", [128, 384], BF16)

    w_in_b = dint("w_in_b", [2, 10, 128, 8, 128], BF16).ap()
    w_in_bc = dint("w_in_bc", [2, D, INW], BF16).ap()
    w_out_b = dint("w_out_b", [2, D, D], BF16).ap()
    w1_b = dint("w1_b", [2, D, DFF], BF16).ap()
    w2_b = dint("w2_b", [2, DFF, D], BF16).ap()
    wp_b = dint("wp_b", [2, 4, 128, 128], BF16).ap()
    hT_d, qT_d, uT_d, umT_d = {}, {}, {}, {}
    kT_send, v_send, kT_gath, v_gath, kT_meta, v_meta, e_send, e_gath = {}, {}, {}, {}, {}, {}, {}, {}
    for l in range(2):
        for sl in SL:
            n, R = sl["n"], sl["R"]
            k = (l, n)
            hT_d[k] = dint(f"hT{l}{n}", [8, 128, NT], F32).ap()
            qT_d[k] = dint(f"qT{l}{n}", [4, 128, NT], BF16).ap()
            uT_d[k] = dint(f"uT{l}{n}", [4, 128, TS], F32).ap()
            umT_d[k] = dint(f"umT{l}{n}", [4, 128, 16], F32).ap()
            kT_send[k] = dint(f"kTs{l}{n}", [128, TS], BF16)
            v_send[k] = dint(f"vs{l}{n}", [TS, 130], BF16)
            kT_gath[k] = dint(f"kTg{l}{n}", [R * 128, TS], BF16)
            v_gath[k] = dint(f"vg{l}{n}", [R * TS, 130], BF16)
            kT_meta[k] = dint(f"kTm{l}{n}", [128, 16], BF16).ap()
            v_meta[k] = dint(f"vm{l}{n}", [16, 130], BF16).ap()
            e_send[k] = dint(f"es{l}{n}", [512, 16], F32)
            e_gath[k] = dint(f"eg{l}{n}", [R * 512, 16], F32)

    rec_d = dint("rec_d", [2, 512], F32).ap()
    es = ExitStack()
    sbtot = [0]

    def sb(name, shape, dt):
        n_ = 1
        for d_ in shape[1:]:
            n_ *= d_
        sbtot[0] += n_ * (2 if dt == BF16 else 4)
        return es.enter_context(nc.sbuf_tensor("s_" + name, list(shape), dt))

    ident = sb("ident", [128, 128], F32)
    rt_sb = sb("rt_sb", [128, 128], F32)
    cbf = sb("cbf", [128, 384], BF16)
    par = sb("par", [128, NPAR], F32)
    icnt = sb("icnt", [128, 2, 4, 32], F32)
    onesD = cbf[:, 0:128]
    blk64 = cbf[:, 128:256]
    ones_f = sb("ones_f", [128, 64], F32)
    KT = sb("KT", [128, 4 * TS + 128], BF16)
    VX = sb("VX", [128, 4 * TS // 128 + 1, 130], BF16)
    wout_p = sb("wout_p", [128, 4, D], BF16)
    wout_a = sb("wout_a", [64, 8, D], BF16)
    wpool = sb("wpool", [128, 4, 128], BF16)
    w1f = [sb(f"w1r{i}", [128, 2048], BF16) for i in range(2)]
    w2f = [sb(f"w2r{i}", [128, 2048], BF16) for i in range(4)]
    w1r = [t_[:].rearrange("p (k f) -> p k f", f=256) for t_ in w1f]
    w2r = [t_[:].rearrange("p (j d) -> p j d", d=D) for t_ in w2f]
    WIN_SLOT = [(w1f[0], "w1r0", 0), (w1f[0], "w1r0", 1), (w1f[1], "w1r1", 0), (w1f[1], "w1r1", 1),
                (w2f[0], "w2r0", 0), (w2f[0], "w2r0", 1), (w2f[1], "w2r1", 0), (w2f[1], "w2r1", 1),
                (w2f[2], "w2r2", 0), (w2f[2], "w2r2", 1)]
    hT = sb("hT", [128, 8, 512], F32)
    qTb = [sb(f"qTb{i}", [128, 4, 512], BF16) for i in range(2)]
    ublk = sb("ublk", [128, 4, 528], F32)
    ptmp = [sb(f"ptmp{i}", [128, 528], F32) for i in range(2)]
    dT = sb("dT", [128, 4, 512], BF16)
    pyT = sb("pyT", [128, 4, 512], BF16)
    PT = [sb(f"PT{i}", [128, 2, 512], BF16) for i in range(3)]
    attnT = sb("attnT", [64, 8, 512], BF16)
    ocp = sb("ocp", [128, 2, 512], F32)
    bcs = sb("bcs", [64, 2, 512], F32)
    nT = sb("nT", [128, 8, 512], BF16)
    scr8 = sb("scr8", [128, 8, 512], BF16)
    rstd = sb("rstd", [128, 512], F32)
    ustgs = [sb(f"ustg{i}", [128, 512], F32) for i in range(2)]
    qstg = [sb(f"qstg{i}", [128, 512], BF16) for i in range(2)]
    vstg = sb("vstg", [128, 4, 130], BF16)
    qsqs = [sb(f"qsq{i}", [128, 512], BF16) for i in range(2)]
    qys = [sb(f"qy{i}", [128, 512], F32) for i in range(2)]
    qt = [sb(f"qt{i}", [128, 512], F32) for i in range(2)]
    ropeC = sb("ropeC", [128, 512], F32)
    ropeS = sb("ropeS", [128, 512], F32)
    xstgs = [sb(f"xstg{i}", [128, 1, D], F32) for i in range(2)]
    EG = sb("EG", [128, 4, 4, 16], F32)
    UM = sb("UM", [128, 4, 16], F32)
    HL = sb("HL", [128, 4, 8], F32)
    HR = sb("HR", [128, 4, 8], F32)
    ps = es.enter_context(nc.psum_tensor("ps", [128, 8, 512], F32))

    def P_n1(l, kc): return par[:, l * 8 + kc: l * 8 + kc + 1]
    def P_n2(l, kc): return par[:, 16 + l * 8 + kc: 16 + l * 8 + kc + 1]
    def P_ps(l, g): return par[:, 32 + l * 4 + g: 32 + l * 4 + g + 1]
    def P_qg(l): return par[:, 40 + l: 41 + l]
    def P_kg(l): return par[:, 42 + l: 43 + l]
    MOFF = {"p": 44, "s": 44 + 9}
    def P_mL(n, r): return par[:, MOFF[n] + r: MOFF[n] + r + 1]
    def P_mR(n, R, r): return par[:, MOFF[n] + R + r: MOFF[n] + R + r + 1]
    P_eps = par[:, 63:64]

    def mm(out, lhsT, rhs, start, stop, reads, writes):
        return S.add("pe", lambda e: e.matmul(out, lhsT, rhs, start=start, stop=stop), reads, writes)

    def tr(out, in_, idn, reads, writes):
        return S.add("pe", lambda e: e.transpose(out, in_, idn), reads, writes)

    def act(out, in_, func, reads, writes, scale=None, bias=None):
        kw = {}
        if scale is not None:
            kw["scale"] = scale
        if bias is not None:
            kw["bias"] = bias
        return S.add("act", lambda e: e.activation(out, in_, func, **kw), reads, writes)

    def tt(eng, out, in0, in1, op, reads, writes):
        return S.add(eng, lambda e: e.tensor_tensor(out, in0, in1, op), reads, writes)

    def ts(eng, out, in0, s1, op0, reads, writes):
        return S.add(eng, lambda e: e.tensor_scalar(out, in0, s1, None, op0), reads, writes)

    def stt(out, in0, scalar, in1, op0, op1, reads, writes):
        return S.add("dve", lambda e: e.scalar_tensor_tensor(out, in0, scalar, in1, op0, op1), reads, writes)

    def cp(eng, out, in_, reads, writes):
        return S.add(eng, lambda e: e.tensor_copy(out, in_), reads, writes)

    def mset(eng, ap, val, writes):
        return S.add(eng, lambda e: e.memset(ap, val), (), writes)

    def dma(q, out, in_, reads, writes, key, pwrites=(), **kw):
        return S.add(q, lambda e: e.dma_start(out=out, in_=in_, **kw), reads, writes, dma_key=key, pwrites=pwrites)

    dma("sp", ident[:], ident_in, (), ("ident",), "c_id")
    dma("sp", rt_sb[:], rt_in, (), ("rt",), "c_rt")
    dma("sp", cbf[:], cb_in, (), ("cbf",), "c_cb")
    dma("sp", par[:], par_in, (), ("par",), "c_par")
    dma("sp", icnt[:], icnt_in.rearrange("s p g c -> p s g c"), (), ("icnt",), "c_ic")
    mset("dve", ones_f[:], 1.0, ("ones_f",))
    mset("dve", vstg[:], 1.0, ("vstg",))
    mset("dve", nT[:], 0.0, tuple(f"nT{kc}" for kc in range(8)))

    cast_q = []

    def cast_rows(dst, src, rows, key, wname):
        ncol = src.shape[-1]
        step = 2048 if ncol >= 2048 else ncol
        for r0 in range(0, rows, 256):
            r1 = min(rows, r0 + 256)
            for c0 in range(0, ncol, step):
                cast_q.append(lambda r0=r0, r1=r1, c0=c0: dma(
                    "pool", dst[r0:r1, c0:c0 + step], src[r0:r1, c0:c0 + step], (), (), key, pwrites=(wname,)))

    def cast_win(l):
        cast_rows(w_in_bc[l], w_in_f[l], D, f"cw_in{l}", f"w_in_bc{l}")
        for kc in range(8):
            cast_q.append(lambda kc=kc: dma(
                "pool", w_in_b[l, :, :, kc, :], w_in_bc[l, kc * 128:(kc + 1) * 128, :].rearrange("p (j c) -> j p c", c=128),
                (f"w_in_bc{l}",), (), f"rl_in{l}", pwrites=(f"w_in_b{l}",)))

    def cast_rest(l):
        cast_rows(wp_b[l].rearrange("g c e -> (g c) e"), wp_f[l].rearrange("g c e -> (g c) e"), 512, f"cw_p{l}", f"wp_b{l}")
        cast_rows(w_out_b[l], w_out_f[l], D, f"cw_out{l}", f"w_out_b{l}")
        cast_rows(w1_b[l], w1_f[l], D, f"cw1{l}", f"w1_b{l}")
        cast_rows(w2_b[l], w2_f[l], DFF, f"cw2{l}", f"w2_b{l}")

    def pump(k):
        for _ in range(min(k, len(cast_q))):
            cast_q.pop(0)()

    cast_win(0)
    pump(len(cast_q))
    cast_rest(0)
    cast_win(1)
    n_l0 = len(cast_q)
    cast_rest(1)
    n_l1 = len(cast_q) - n_l0

    def load_win_all(l):
        views = []
        for j in range(10):
            t_, reg, h = WIN_SLOT[j]
            v_ = t_[:, h * 1024:(h + 1) * 1024].rearrange("p (k c) -> p k c", c=128)
            dma("sp", v_, w_in_b[l, j], (f"w_in_b{l}",), (), reg, pwrites=(reg,))
            views.append((v_, reg))
        return views

    def load_wout(l):
        dma("sp", wout_p[:], w_out_b[l, 0:512, :].rearrange("(c p) d -> p c d", p=128), (f"w_out_b{l}",), ("wout_p",), "wout_p")
        dma("sp", wout_a[:], w_out_b[l, 512:1024, :].rearrange("(h p) d -> p h d", p=64), (f"w_out_b{l}",), ("wout_a",), "wout_a")
        dma("sp", wpool[:], wp_b[l].rearrange("g c e -> c g e"), (f"wp_b{l}",), ("wpool",), "wpool")

    def rstd_from_ms(out, in_ps, reads, wname):
        act(out, in_ps, AF.Ln, reads + ("par",), (wname,), bias=P_eps)
        act(out, out, AF.Exp, (wname,), (wname,), scale=-0.5)

    def rmsnorm_big(nt, gfun, bank):
        for kc in range(8):
            eng = ("act", "dve", "pool", "act", "dve", "pool", "act", "dve")[kc]
            if eng == "act":
                act(scr8[:, kc, :nt], hT[:, kc, :nt], AF.Square, (f"hT{kc}",), (f"scr8_{kc}",))
            else:
                tt(eng, scr8[:, kc, :nt], hT[:, kc, :nt], hT[:, kc, :nt], ALU.mult, (f"hT{kc}",), (f"scr8_{kc}",))
        for kc in range(8):
            mm(ps[:, bank, :nt], onesD, scr8[:, kc, :nt], kc == 0, kc == 7, (f"scr8_{kc}", "cbf"), (f"ps{bank}",))
        rstd_from_ms(rstd[:, :nt], ps[:, bank, :nt], (f"ps{bank}",), "rstd")
        for kc in range(8):
            stt(nT[:, kc, :nt], hT[:, kc, :nt], gfun(kc), rstd[:, :nt], ALU.mult, ALU.mult,
                (f"hT{kc}", "rstd", "par"), (f"nT{kc}",))

    def a_part(l, n, blk, nt):
        k = (l, n)
        si = 0 if n == "p" else 1
        c0 = 0 if blk < 0 else NMETA + blk * 512
        dma("sp", ropeC[:, :nt], rope_in[si, 0, :, c0:c0 + nt], (), ("ropeC",), "ropeC")
        dma("sp", ropeS[:, :nt], rope_in[si, 1, :, c0:c0 + nt], (), ("ropeS",), "ropeS")
        import os as _o
        KA = int(_o.environ.get("KA", "9"))
        wviews = load_win_all(l)
        dma("sp", hT_d[k][:, :, c0:c0 + nt].rearrange("kc p t -> p kc t"), hT[:, :, :nt],
            tuple(f"hT{kc}" for kc in range(8)), (f"hTd{l}{n}b{blk}",), "hst")
        rmsnorm_big(nt, lambda kc: P_n1(l, kc), 7)
        if KA < 2:
            return
        pb = [0]

        def nextbank():
            b_ = pb[0] % 4
            pb[0] += 1
            return b_

        ubank = [nextbank() for _ in range(4)]
        for kc in range(8):
            for j in range(2):
                wt, wreg = wviews[j]
                mm(ps[:, ubank[j], :nt], wt[:, kc, :], nT[:, kc, :nt], kc == 0, kc == 7, (wreg, f"nT{kc}"), (f"ps{ubank[j]}",))
        for j in range(4):
            b = ubank[j]
            wt, wreg = wviews[j]
            if j >= 2:
                for kc in range(8):
                    mm(ps[:, b, :nt], wt[:, kc, :], nT[:, kc, :nt], kc == 0, kc == 7, (wreg, f"nT{kc}"), (f"ps{b}",))
            ustg = ustgs[j % 2]
            ur = f"ustg{j % 2}"
            act(ustg[:, :nt], ps[:, b, :nt], AF.Copy, (f"ps{b}",), (ur,))
            if blk < 0:
                dma("sp", umT_d[k][j], ustg[:, :16], (ur,), (), f"ust{j % 2}", pwrites=(f"umT{l}{n}",))
            else:
                dma("sp", uT_d[k][j, :, blk * 512:(blk + 1) * 512], ustg[:, :512], (ur,), (), f"ust{j % 2}", pwrites=(f"uT{l}{n}b{blk}",))
                if blk == 0:
                    dma("sp", e_send[k].ap()[j * 128:(j + 1) * 128, 0:8], ustg[:, 0:8], (ur,), (), f"est{j % 2}", pwrites=(f"es{l}{n}",))
                if blk == NB - 1:
                    dma("sp", e_send[k].ap()[j * 128:(j + 1) * 128, 8:16], ustg[:, 504:512], (ur,), (), f"est{j % 2}", pwrites=(f"es{l}{n}",))
        if KA < 3:
            return
        def stage2(j):
            p2 = j % 2
            qsq, qy = qsqs[p2], qys[p2]
            bm, br = 4 + p2, 6 + p2
            mm(ps[:, bm, :nt], blk64, qsq[:, :nt], True, True, (f"qsq{p2}", "cbf"), (f"ps{bm}",))
            mm(ps[:, br, :nt], rt_sb[:], qy[:, :nt], True, True, (f"qy{p2}", "rt"), (f"ps{br}",))
            rstd_from_ms(rstd[:, :nt], ps[:, bm, :nt], (f"ps{bm}",), "rstd")
            tt("pool", qt[0][:, :nt], qy[:, :nt], ropeC[:, :nt], ALU.mult, (f"qy{p2}", "ropeC"), ("qt0",))
            tt("dve", qt[1][:, :nt], ps[:, br, :nt], ropeS[:, :nt], ALU.mult, (f"ps{br}", "ropeS"), ("qt1",))
            tt("pool", qt[0][:, :nt], qt[0][:, :nt], qt[1][:, :nt], ALU.add, ("qt0", "qt1"), ("qt0",))
            st = qstg[p2]
            sreg = f"qstg{p2}"
            tt("dve", st[:, :nt], qt[0][:, :nt], rstd[:, :nt], ALU.mult, ("qt0", "rstd"), (sreg,))
            if j < 4:
                dma("sp", qT_d[k][j, :, c0:c0 + nt], st[:, :nt], (sreg,), (), f"qst{p2}", pwrites=(f"qT{l}{n}b{blk}",))
            elif blk < 0:
                dma("sp", kT_meta[k], st[:, :16], (sreg,), (f"kTm{l}{n}",), f"qst{p2}")
            else:
                dma("sp", kT_send[k].ap()[:, blk * 512:(blk + 1) * 512], st[:, :512], (sreg,), (), f"qst{p2}", pwrites=(f"kTs{l}{n}",))

        for j in range(5):
            b = nextbank()
            p2 = j % 2
            wt, wreg = wviews[4 + j]
            for kc in range(8):
                mm(ps[:, b, :nt], wt[:, kc, :], nT[:, kc, :nt], kc == 0, kc == 7, (wreg, f"nT{kc}"), (f"ps{b}",))
            g = P_qg(l) if j < 4 else P_kg(l)
            act(qsqs[p2][:, :nt], ps[:, b, :nt], AF.Square, (f"ps{b}",), (f"qsq{p2}",))
            ts("dve", qys[p2][:, :nt], ps[:, b, :nt], g, ALU.mult, (f"ps{b}", "par"), (f"qy{p2}",))
            if j > 0:
                stage2(j - 1)
        stage2(4)
        if KA < 4:
            return
        wt, wreg = wviews[9]
        ntile = max(1, nt // 128)
        rows = min(128, nt)
        for t in range(ntile):
            b = nextbank()
            for kc in range(8):
                mm(ps[:, b, 0:128], nT[:, kc, t * 128:t * 128 + 128], wt[:, kc, :], kc == 0, kc == 7,
                   (wreg, f"nT{kc}"), (f"ps{b}",))
            S.add("act", lambda e, t=t, b=b: e.activation(
                vstg[:rows, t, :].rearrange("p (g c) -> p g c", g=2)[:, :, 0:64],
                ps[:rows, b, 0:128].rearrange("p (g c) -> p g c", g=2), AF.Copy), (f"ps{b}", "vstg"), ("vstg",))
        if blk < 0:
            dma("sp", v_meta[k], vstg[:16, 0, :], ("vstg",), (f"vm{l}{n}",), "vst")
        else:
            dma("sp", v_send[k].ap()[blk * 512:(blk + 1) * 512, :].rearrange("(t p) c -> p t c", p=128), vstg[:],
                ("vstg",), (), "vst", pwrites=(f"vs{l}{n}",))

    def exchange(l, n, R):
        k = (l, n)
        groups = [list(range(i, i + R)) for i in range(0, NCORES, R)]
        for (src, dst, rname, wname) in (
            (kT_send[k], kT_gath[k], f"kTs{l}{n}", f"kTg{l}{n}"),
            (v_send[k], v_gath[k], f"vs{l}{n}", f"vg{l}{n}"),
            (e_send[k], e_gath[k], f"es{l}{n}", f"eg{l}{n}"),
        ):
            S.add("pool", lambda e, src=src, dst=dst: e.collective_compute(
                "AllGather", ALU.bypass, replica_groups=groups,
                ins=[src.ap().opt()], outs=[dst.ap().opt()]), (rname,), (wname,), dma_key=f"cc_{wname}", inc=1)

    def load_kv(l, n, R):
        k = (l, n)
        for r in range(R):
            dma("sp", KT[:, r * TS:(r + 1) * TS], kT_gath[k].ap()[r * 128:(r + 1) * 128, :], (f"kTg{l}{n}",), (), "kvl_k", pwrites=("KT",))
        S.add("dve", lambda e: e.memset(KT[:, R * TS:R * TS + 128], 0.0), (), ("KTm",), pwrites=("KT",))
        dma("sp", KT[:, R * TS:R * TS + 16], kT_meta[k], (f"kTm{l}{n}",), ("KTm",), "kvl_km")
        for r in range(R):
            dma("sp", VX[:, r * (TS // 128):(r + 1) * (TS // 128), :],
                v_gath[k].ap()[r * TS:(r + 1) * TS, :].rearrange("(i p) c -> p i c", p=128), (f"vg{l}{n}",), (), "kvl_v", pwrites=("VX",))
        S.add("dve", lambda e: e.memset(VX[:, R * TS // 128, :], 0.0), (), ("VXm",), pwrites=("VX",))
        dma("sp", VX[:16, R * TS // 128, :], v_meta[k], (f"vm{l}{n}",), ("VXm",), "kvl_vm")
        dma("sp", EG[:, :R, :, :], e_gath[k].ap().rearrange("(r g p) c -> p r g c", g=4, p=128), (f"eg{l}{n}",), ("EG",), "kvl_e")
        dma("sp", UM[:], umT_d[k].rearrange("g p c -> p g c"), (f"umT{l}{n}",), ("UM",), "kvl_u")
        ts("dve", HL[:], UM[:, :, 8:16], P_mL(n, 0), ALU.mult, ("UM", "par"), ("HL",))
        for r in range(1, R):
            stt(HL[:], EG[:, r - 1, :, 8:16], P_mL(n, r), HL[:], ALU.mult, ALU.add, ("EG", "par", "HL"), ("HL",))
        ts("dve", HR[:], EG[:, 1, :, 0:8], P_mR(n, R, 0), ALU.mult, ("EG", "par"), ("HR",))
        for r in range(1, R - 1):
            stt(HR[:], EG[:, r + 1, :, 0:8], P_mR(n, R, r), HR[:], ALU.mult, ALU.add, ("EG", "par", "HR"), ("HR",))

    def b_loads(l, n, blk, nt, qi):
        k = (l, n)
        c0 = 0 if blk < 0 else NMETA + blk * 512
        dma("sp", qTb[qi][:, :, :nt], qT_d[k][:, :, c0:c0 + nt].rearrange("c p t -> p c t"), (f"qT{l}{n}b{blk}",), (f"qTb{qi}",), f"qld{qi}")

    def pool_mix(l, n, R, blk, nt):
        k = (l, n)
        si = 0 if n == "p" else 1
        W = nt + 16
        if blk < 0:
            mset("dve", ublk[:, :, 0:8], 0.0, ("ublk",))
            cp("dve", ublk[:, :, 8:24], UM[:], ("UM", "ublk"), ("ublk",))
            cp("dve", ublk[:, :, 24:32], EG[:, 0, :, 0:8], ("EG", "ublk"), ("ublk",))
        else:
            lo = max(blk * 512 - 8, 0)
            hi = min(blk * 512 + 520, TS)
            o = lo - (blk * 512 - 8)
            dma("sp", ublk[:, :, o:o + hi - lo], uT_d[k][:, :, lo:hi].rearrange("g p t -> p g t"),
                tuple(f"uT{l}{n}b{bb}" for bb in range(NB)), ("ublk",), "uld")
            if blk == 0:
                cp("dve", ublk[:, :, 0:8], HL[:], ("HL", "ublk"), ("ublk",))
            if blk == NB - 1:
                cp("dve", ublk[:, :, 520:528], HR[:], ("HR", "ublk"), ("ublk",))
        for g, w in enumerate((2, 4, 8, 16)):
            hw = w // 2
            src = ublk[:, g, :]
            ln = W
            sh = 1
            ti = 0
            sname = "ublk"
            while sh < w:
                ln2 = ln - sh
                dst = ptmp[ti]
                tt("dve", dst[:, :ln2], src[:, 0:ln2], src[:, sh:sh + ln2], ALU.add, (sname,), (f"ptmp{ti}",))
                src, sname, ln = dst, f"ptmp{ti}", ln2
                ti ^= 1
                sh *= 2
            off = 8 - hw
            if blk < 0:
                tab = icnt[:, si, g, 0:16]
                o2 = ptmp[ti]
                tt("dve", o2[:, :nt], src[:, off:off + nt], tab, ALU.mult, (sname, "icnt"), (f"ptmp{ti}",))
                tt("dve", dT[:, g, :nt], o2[:, :nt], ublk[:, g, 8:8 + nt], ALU.subtract, (f"ptmp{ti}", "ublk"), (f"dT{g}",))
            else:
                n1 = nt - 16 if blk == NB - 1 else nt
                stt(dT[:, g, :n1], src[:, off:off + n1], 1.0 / w, ublk[:, g, 8:8 + n1], ALU.mult, ALU.subtract,
                    (sname, "ublk"), (f"dT{g}",))
                if blk == NB - 1:
                    tab = icnt[:, si, g, 16:32]
                    o2 = ptmp[ti]
                    tt("dve", o2[:, :16], src[:, off + n1:off + nt], tab, ALU.mult, (sname, "icnt"), (f"ptmp{ti}",))
                    tt("dve", dT[:, g, n1:nt], o2[:, :16], ublk[:, g, 8 + n1:8 + nt], ALU.subtract,
                       (f"ptmp{ti}", "ublk", f"dT{g}"), (f"dT{g}",))

    def pool_pe(l, nt):
        for g in range(4):
            b = g % 2
            mm(ps[:, b, :nt], wpool[:, g, :], dT[:, g, :nt], True, True, ("wpool", f"dT{g}"), (f"ps{b}",))
            act(pyT[:, g, :nt], ps[:, b, :nt], AF.Copy, (f"ps{b}", "par"), (f"pyT{g}",), scale=P_ps(l, g))

    def attention(l, n, R, nt, qi):
        nfull = R * TS // 128
        ntile = nfull + 1
        q = qTb[qi]
        pending = []
        for c in range(4):
            heads = (c, c + 4)

            def qk(i, c=c):
                slot = i % 3
                for hh in range(2):
                    b = slot * 2 + hh
                    mm(ps[:, b, :nt], KT[64 * hh:64 * hh + 64, i * 128:i * 128 + 128], q[64 * hh:64 * hh + 64, c, :nt],
                       True, True, ("KT", "KTm", f"qTb{qi}"), (f"ps{b}",))

            qk(0)
            qk(1)
            for i in range(ntile):
                slot = i % 3
                if i + 2 < ntile:
                    qk(i + 2)
                act(PT[slot][:, :, :nt], ps[:, slot * 2:slot * 2 + 2, :nt], AF.Exp,
                    (f"ps{slot * 2}", f"ps{slot * 2 + 1}"), (f"PT{slot}",), scale=0.125)
                if i == 1 and pending:
                    pending.pop()()
                for hh in range(2):
                    mm(ps[:65, 6 + hh, :nt], VX[:, i, hh * 65:hh * 65 + 65], PT[slot][:, hh, :nt],
                       i == 0, i == ntile - 1, ("VX", "VXm", f"PT{slot}"), (f"ps{6 + hh}",))
            for hh in range(2):
                cp("dve", ocp[:65, hh, :nt], ps[:65, 6 + hh, :nt], (f"ps{6 + hh}",), (f"ocp{hh}", f"ocpd{hh}"))

            def epilogue(heads=heads, c=c):
                if c < 3:
                    S.add("dve", lambda e: e.reciprocal(ocp[64:65, :, :nt], ocp[64:65, :, :nt]),
                          ("ocpd0", "ocpd1"), ("ocpd0", "ocpd1"))
                else:
                    act(ocp[64:65, :, :nt], ocp[64:65, :, :nt], AF.Ln, ("ocpd0", "ocpd1"), ("ocpd0", "ocpd1"))
                    act(ocp[64:65, :, :nt], ocp[64:65, :, :nt], AF.Exp, ("ocpd0", "ocpd1"), ("ocpd0", "ocpd1"), scale=-1.0)
                dma("sp", rec_d[:, :nt], ocp[64:65, :, :nt], ("ocpd0", "ocpd1"), ("recd",), "bcd0")
                dma("sp", bcs[:, :, :nt], rec_d[:, :nt].partition_broadcast(64), ("recd",), ("bcs",), "bcd1")
                for hh in range(2):
                    tt("dve", attnT[:, heads[hh], :nt], ocp[:64, hh, :nt], bcs[:, hh, :nt], ALU.mult,
                       (f"ocp{hh}", "bcs"), (f"attnT{heads[hh]}",))

            pending.append(epilogue)
        pending.pop()()

    def out_proj(nt):
        for j in range(8):
            b = j % 2
            for i in range(4):
                mm(ps[:, b, :nt], wout_p[:, i, j * 128:(j + 1) * 128], pyT[:, i, :nt], i == 0, False,
                   ("wout_p", f"pyT{i}"), (f"ps{b}",))
            horder = (0, 4, 1, 5, 2, 6, 3, 7)
            for hi, h in enumerate(horder):
                mm(ps[:, b, :nt], wout_a[:, h, j * 128:(j + 1) * 128], attnT[:, h, :nt], False, hi == 7,
                   ("wout_a", f"attnT{h}"), (f"ps{b}",))
            tt("dve", hT[:, j, :nt], ps[:, b, :nt], hT[:, j, :nt], ALU.add, (f"ps{b}", f"hT{j}"), (f"hT{j}",))

    wctr = [0, 0]

    def mlp(l, nt):
        w1slot, w2slot = {}, {}

        def issue_w1(gi):
            if gi in w1slot or gi > 15:
                return
            s1 = wctr[0] % 2
            wctr[0] += 1
            dma("sp", w1r[s1], w1_b[l][:, gi * 256:(gi + 1) * 256].rearrange("(kc p) f -> p kc f", p=128),
                (f"w1_b{l}",), (f"w1r{s1}",), f"w1r{s1}")
            w1slot[gi] = s1

        def issue_w2(gi):
            if gi in w2slot or gi > 15:
                return
            s2 = wctr[1] % 4
            wctr[1] += 1
            dma("sp", w2r[s2], w2_b[l][gi * 256:(gi + 1) * 256, :].rearrange("(j p) d -> p j d", p=128),
                (f"w2_b{l}",), (f"w2r{s2}",), f"w2r{s2}")
            w2slot[gi] = s2

        issue_w1(0)
        issue_w1(1)
        rmsnorm_big(nt, lambda kc: P_n2(l, kc), 7)
        for qtr in range(4):
            for fg in range(4):
                gi = qtr * 4 + fg
                issue_w1(gi)
                issue_w1(gi + 1)
                if fg == 2:
                    issue_w2(qtr * 4)
                    issue_w2(qtr * 4 + 1)
                s1 = w1slot[gi]
                if qtr == 0 and fg == 0:
                    for kc in range(8):
                        for ff in range(2):
                            mm(ps[:, 2 + ff, :nt], w1r[s1][:, kc, ff * 128:(ff + 1) * 128], nT[:, kc, :nt], kc == 0, kc == 7,
                               (f"w1r{s1}", f"nT{kc}"), (f"ps{2 + ff}",))
                for ff in range(2):
                    fc = fg * 2 + ff
                    b = 2 + (fc % 2)
                    if not (qtr == 0 and fg == 0):
                        for kc in range(8):
                            mm(ps[:, b, :nt], w1r[s1][:, kc, ff * 128:(ff + 1) * 128], nT[:, kc, :nt], kc == 0, kc == 7,
                               (f"w1r{s1}", f"nT{kc}"), (f"ps{b}",))
                    rb = qt[fc % 2]
                    act(rb[:, :nt], ps[:, b, :nt], AF.Relu, (f"ps{b}",), (f"qt{fc % 2}",))
                    tt("pool", scr8[:, fc, :nt], rb[:, :nt], rb[:, :nt], ALU.mult, (f"qt{fc % 2}",), (f"scr8_{fc}",))
            for half in range(2):
                g0 = qtr * 4 + half * 2
                issue_w2(g0)
                issue_w2(g0 + 1)
                issue_w2(g0 + 2)
                issue_w2(g0 + 3)
                sl2 = [w2slot[g0], w2slot[g0 + 1]]
                for j in range(8):
                    b = j % 2
                    for fg in range(2):
                        for ff in range(2):
                            fc = half * 4 + fg * 2 + ff
                            mm(ps[:, b, :nt], w2r[sl2[fg]][:, ff, j * 128:(j + 1) * 128], scr8[:, fc, :nt],
                               fg == 0 and ff == 0, fg == 1 and ff == 1, (f"w2r{sl2[fg]}", f"scr8_{fc}"), (f"ps{b}",))
                    tt("dve", hT[:, j, :nt], ps[:, b, :nt], hT[:, j, :nt], ALU.add, (f"ps{b}", f"hT{j}"), (f"hT{j}",))

    x_issued = set()

    def x_dma(n, blk, q4):
        if blk < 0 or (n, blk, q4) in x_issued:
            return
        x_issued.add((n, blk, q4))
        r0 = blk * 512 + q4 * 128
        dma("sp", xstgs[q4 % 2][:, 0, :], x_in[n][r0:r0 + 128, :], (), (f"xstg{q4 % 2}",), f"xld{q4 % 2}")

    def load_x(n, blk, nt):
        if blk < 0:
            dma("sp", hT[:, :, 0:16], meta_in.rearrange("(kc p) t -> p kc t", p=128), (),
                tuple(f"hT{kc}" for kc in range(8)), "hld")
            return
        x_dma(n, blk, 0)
        x_dma(n, blk, 1)
        for q4 in range(4):
            xstg = xstgs[q4 % 2]
            xr = f"xstg{q4 % 2}"
            for kc in range(8):
                b = kc % 2
                tr(ps[:, b, 0:128], xstg[:, 0, kc * 128:(kc + 1) * 128], ident[:], (xr, "ident"), (f"ps{b}",))
                act(hT[:, kc, q4 * 128:(q4 + 1) * 128], ps[:, b, 0:128], AF.Copy, (f"ps{b}", f"hT{kc}"), (f"hT{kc}",))
            if q4 + 2 < 4:
                x_dma(n, blk, q4 + 2)

    def store_y(n, blk, nt):
        for q4 in range(4):
            tok = q4 * 128
            xstg = xstgs[q4 % 2]
            xr = f"xstg{q4 % 2}"
            for kc in range(8):
                b = 2 + (kc // 4) % 2
                tr(ps[:, b, (kc % 4) * 128:(kc % 4) * 128 + 128], hT[:, kc, tok:tok + 128], ident[:],
                   (f"hT{kc}", "ident"), (f"ps{b}",))
                if kc % 4 == 3:
                    act(xstg[:, 0, (kc // 4) * 512:(kc // 4) * 512 + 512], ps[:, b, :], AF.Copy, (f"ps{b}", xr), (xr,))
            r0 = blk * 512 + tok
            dma("sp", y_out[n][r0:r0 + 128, :], xstg[:, 0, :], (xr,), (), f"yst{q4 % 2}", pwrites=(f"y{n}",))

    blocks = [(-1, NMETA)] + [(b, 512) for b in range(NB)]
    import os as _os
    KSTOP = int(_os.environ.get("KSTOP", "9"))
    for sl in SL:
        n, R = sl["n"], sl["R"]
        for bi0, (blk, nt) in enumerate(blocks):
            if KSTOP >= 1 and not (blk < 0 and _os.environ.get("KNOMETA")):
                load_x(n, blk, nt)
                if bi0 + 1 < len(blocks):
                    x_dma(n, blocks[bi0 + 1][0], 0)
                    x_dma(n, blocks[bi0 + 1][0], 1)
            if KSTOP >= 2 and not (blk < 0 and _os.environ.get("KNOMETA")):
                a_part(0, n, blk, nt)
            pump(6 if blk >= 0 else 2)
        if KSTOP >= 3:
            exchange(0, n, R)
    pump(max(0, len(cast_q) - n_l1))
    qi = 0
    order = [(l, sl["n"], sl["R"]) for l in range(2 if KSTOP >= 4 else 0) for sl in SL]
    for oi, (l, n, R) in enumerate(order):
        if n == "p":
            load_wout(l)
        if True:
            if oi == 0:
                load_kv(l, n, R)
            blks = blocks if l == 0 else blocks[1:]
            b_loads(l, n, blks[0][0], blks[0][1], qi)
            for bi, (blk, nt) in enumerate(blks):
                c0 = 0 if blk < 0 else NMETA + blk * 512
                cur = qi
                if bi + 1 < len(blks):
                    b_loads(l, n, blks[bi + 1][0], blks[bi + 1][1], 1 - qi)
                dma("sp", hT[:, :, :nt], hT_d[(l, n)][:, :, c0:c0 + nt].rearrange("kc p t -> p kc t"),
                    (f"hTd{l}{n}b{blk}",), tuple(f"hT{kc}" for kc in range(8)), "hld")
                pool_mix(l, n, R, blk, nt)
                attention(l, n, R, nt, cur)
                if bi == len(blks) - 1 and oi + 1 < len(order):
                    load_kv(*order[oi + 1])
                pool_pe(l, nt)
                out_proj(nt)
                mlp(l, nt)
                if l == 0:
                    a_part(1, n, blk, nt)
                    pump(2)
                elif blk >= 0:
                    store_y(n, blk, nt)
                qi = 1 - qi
            if l == 0:
                exchange(1, n, R)
                if n == "s":
                    pump(len(cast_q))

    names = S.finalize()
    S.final = [("dma:" + k_, (1 if k_.startswith("cc_") else 16) * v_) for k_, v_ in S.dma_cnt.items()]
    print("kernel build: sbuf bytes/partition", sbtot[0])
    print("kernel build: ops", {e: len(S.ops[e]) for e in ENGS}, "sems", len(names))
    sems = {nm: es.enter_context(nc.semaphore(nm.replace(":", "_"))) for nm in names}
    with nc.Block() as block:
        @block.tensor
        def _(e):
            S.emit("pe", e, sems)

        @block.scalar
        def _(e):
            S.emit("act", e, sems)

        @block.vector
        def _(e):
            S.emit("dve", e, sems)

        @block.gpsimd
        def _(e):
            S.emit("pool", e, sems)

        @block.sync
        def _(e):
            S.emit("sp", e, sems)
    es.close()
    return nc


def _rope_tables(n_tok_total, tok0, ntok):
    inv = (10000.0 ** (-np.arange(16, dtype=np.float32) / 16)).astype(np.float32)
    t = np.arange(tok0, tok0 + ntok)
    r = np.concatenate([np.zeros(NMETA), t // GRID_W]).astype(np.float32)
    c = np.concatenate([np.zeros(NMETA), t % GRID_W]).astype(np.float32)
    p = np.arange(128)
    j = p % 64
    axis = j // 32
    fi = j % 16
    pos = np.where(axis[:, None] == 0, r[None, :], c[None, :]).astype(np.float32)
    ang = (pos * inv[fi][:, None]).astype(np.float32)
    return np.cos(ang).astype(np.float32), np.sin(ang).astype(np.float32)


def _consts():
    ident = np.eye(128, dtype=np.float32)
    R = np.zeros((128, 128), np.float32)
    for m in range(128):
        if (m % 32) < 16:
            R[m, m + 16] = -1.0
        else:
            R[m, m - 16] = 1.0
    rt = np.ascontiguousarray(R.T)
    cb = np.zeros((128, 384), np.float32)
    cb[:, 0:128] = 1.0 / D
    for h in range(2):
        cb[64 * h:64 * h + 64, 128 + 64 * h:128 + 64 * h + 64] = 1.0 / 64
    cb[:, 256:384] = 1.0
    return ident, rt, cb.astype(ml_dtypes.bfloat16)


_NC_CACHE = {}


def _run(inputs, TS):
    f = lambda a: np.ascontiguousarray(np.asarray(a, dtype=np.float32))
    xp, xs = f(inputs["x_prompt"]), f(inputs["x_sample"])
    w_in = f(inputs["w_in"])
    perm = list(range(512))
    for c in range(4):
        perm += list(range(512 + c * 64, 512 + c * 64 + 64)) + list(range(512 + (c + 4) * 64, 512 + (c + 4) * 64 + 64))
    perm += list(range(1024, 1280))
    w_in = np.ascontiguousarray(w_in[:, :, perm])
    n1, n2 = f(inputs["norm1_g"]), f(inputs["norm2_g"])
    psc = f(inputs["pool_scale"])
    qg, kg = f(inputs["q_norm_g"]), f(inputs["k_norm_g"])
    ident, rt, cb = _consts()
    Lp, Ls = 4 * TS + NMETA, 2 * TS + NMETA
    in_maps = []
    for c in range(NCORES):
        pb, pq, sbi, sh = c // 4, c % 4, c // 2, c % 2
        par = np.zeros((128, 64), np.float32)
        for l in range(2):
            par[:, l * 8:(l + 1) * 8] = n1[l].reshape(8, 128).T
            par[:, 16 + l * 8:16 + (l + 1) * 8] = n2[l].reshape(8, 128).T
            par[:, 32 + l * 4:32 + (l + 1) * 4] = psc[l].reshape(4, 128).T
            par[:, 40 + l] = np.tile(qg[l], 2)
            par[:, 42 + l] = np.tile(kg[l], 2)
        for (off, R, rk) in ((44, 4, pq), (53, 2, sh)):
            par[:, off + rk] = 1.0
            if rk < R - 1:
                par[:, off + R + rk] = 1.0
        rope = np.zeros((2, 2, 128, NMETA + TS), np.float32)
        rope[0, 0], rope[0, 1] = _rope_tables(4 * TS, pq * TS, TS)
        rope[1, 0], rope[1, 1] = _rope_tables(2 * TS, sh * TS, TS)
        icnt = np.zeros((2, 128, 4, 32), np.float32)
        for si, (L, R, rk) in enumerate(((Lp, 4, pq), (Ls, 2, sh))):
            for g, w in enumerate((2, 4, 8, 16)):
                t = np.arange(16)
                lo = np.clip(t - w // 2, 0, L)
                hi = np.clip(t + (w - w // 2), 0, L)
                icnt[si, :, g, 0:16] = (1.0 / (hi - lo).astype(np.float32))[None, :]
                if rk == R - 1:
                    t = np.arange(L - 16, L)
                    lo = np.clip(t - w // 2, 0, L)
                    hi = np.clip(t + (w - w // 2), 0, L)
                    icnt[si, :, g, 16:32] = (1.0 / (hi - lo).astype(np.float32))[None, :]
                else:
                    icnt[si, :, g, 16:32] = 1.0 / w
        par[:, 63] = EPS
        in_maps.append({
            "xp": np.ascontiguousarray(xp[pb, pq * TS:(pq + 1) * TS]),
            "xs": np.ascontiguousarray(xs[sbi, sh * TS:(sh + 1) * TS]),
            "metaT": np.ascontiguousarray(f(inputs["meta_tokens"]).T),
            "w_in": w_in, "w_out": f(inputs["w_out"]), "w1": f(inputs["w_mlp_in"]), "w2": f(inputs["w_mlp_out"]),
            "w_pool": f(inputs["w_pool"]),
            "params": par, "rope": rope, "icnt": icnt, "ident": ident, "rt": rt, "cbf": cb,
        })
    if TS not in _NC_CACHE:
        _NC_CACHE[TS] = build(TS)
    nc = _NC_CACHE[TS]
    res = run_bass_kernel_spmd(nc, in_maps, core_ids=list(range(NCORES)))
    yp = np.zeros((2, 4 * TS, D), np.float32)
    ys = np.zeros((4, 2 * TS, D), np.float32)
    for c in range(NCORES):
        pb, pq, sbi, sh = c // 4, c % 4, c // 2, c % 2
        yp[pb, pq * TS:(pq + 1) * TS] = res.results[c]["yp"]
        ys[sbi, sh * TS:(sh + 1) * TS] = res.results[c]["ys"]
    return yp, ys


def kernel(**inputs):
    TS = inputs["x_prompt"].shape[1] // 4
    return _run(inputs, TS)
```
